# Optimizing a Trainium2 kernel written in Bass

```python
import math
import jax, jax.numpy as jnp
from jax import lax
import numpy as np

D_MODEL = 1024
BATCH = 1
SEQ = 16384
DEPTH = 4

HEAD_DIM = 64
NA_HEADS = 8
DIFF_HEADS = 4
GQA_Q_HEADS = 8
GQA_KV_HEADS = 2
BRANCH_WIDTH = 512
N_BRANCHES = 3
D_FF = 2816
GRID_W = 64
NA_WIN_ROWS = 8
NA_WIN_COLS = 16
WINDOW = 128
BLOCK = 128
T5_BUCKETS = 32
T5_MAX_DIST = 128
T5_HEADS = DIFF_HEADS + GQA_Q_HEADS
NEG_INF = -1e30
EPS = 1e-6

PROJ_SPLITS = (
    NA_HEADS * HEAD_DIM, NA_HEADS * HEAD_DIM, NA_HEADS * HEAD_DIM,
    DIFF_HEADS * 2 * HEAD_DIM, DIFF_HEADS * 2 * HEAD_DIM, DIFF_HEADS * 2 * HEAD_DIM,
    GQA_Q_HEADS * HEAD_DIM, GQA_KV_HEADS * HEAD_DIM, GQA_KV_HEADS * HEAD_DIM,
)
W_IN_COLS = sum(PROJ_SPLITS)
PROJ_OFFSETS = tuple(sum(PROJ_SPLITS[:i + 1]) for i in range(len(PROJ_SPLITS) - 1))

kernel_name = "hybrid_parallel_gated_encoder"


def rms_norm(x, g):
    xf = x.astype(jnp.float32)
    y = xf * lax.rsqrt(jnp.mean(xf * xf, axis=-1, keepdims=True) + EPS)
    return (y * g.astype(jnp.float32)).astype(x.dtype)


def swiglu(h, w_gate, w_up, w_down):
    return (jax.nn.silu(h @ w_gate) * (h @ w_up)) @ w_down


def t5_bucket(rel):
    half = T5_BUCKETS // 2
    max_exact = half // 2
    ret = (rel > 0).astype(jnp.int32) * half
    n = jnp.abs(rel)
    nf = jnp.maximum(n, 1).astype(jnp.float32)
    large = max_exact + (jnp.log(nf / max_exact) / math.log(T5_MAX_DIST / max_exact)
                         * (half - max_exact)).astype(jnp.int32)
    large = jnp.minimum(large, half - 1)
    return ret + jnp.where(n < max_exact, n, large)


def neighbourhood_attention(q, k, v, rpb):
    B, S, H, D = q.shape
    rows = S // GRID_W
    wr = min(NA_WIN_ROWS, rows)
    qg = q.reshape(B, rows, GRID_W, H, D)
    kg = k.reshape(B, rows, GRID_W, H, D)
    vg = v.reshape(B, rows, GRID_W, H, D)
    r = jnp.arange(rows)
    rs = jnp.clip(r - wr // 2, 0, rows - wr)
    row_idx = rs[:, None] + jnp.arange(wr)[None, :]
    kn = kg[:, row_idx]
    vn = vg[:, row_idx]
    s = jnp.einsum('brqhd,brikhd->bhrqik', qg, kn) * (D ** -0.5)
    c = jnp.arange(GRID_W)
    cs = jnp.clip(c - NA_WIN_COLS // 2, 0, GRID_W - NA_WIN_COLS)
    col_ok = (c[None, :] >= cs[:, None]) & (c[None, :] < cs[:, None] + NA_WIN_COLS)
    dr = row_idx - r[:, None] + (NA_WIN_ROWS - 1)
    dc = jnp.clip(c[None, :] - c[:, None] + (NA_WIN_COLS - 1), 0, 2 * NA_WIN_COLS - 2)
    bias = rpb[:, dr[:, None, :, None], dc[None, :, None, :]]
    logits = s.astype(jnp.float32) + bias.astype(jnp.float32)[None]
    logits = jnp.where(col_ok[:, None, :], logits, NEG_INF)
    p = jax.nn.softmax(logits.reshape(B, H, rows, GRID_W, wr * GRID_W), axis=-1)
    p = p.reshape(logits.shape).astype(v.dtype)
    o = jnp.einsum('bhrqik,brikhd->brqhd', p, vn)
    return o.reshape(B, S, H * D)


def diff_attention(q, k, v, lam, lam_init, subln_g, bias_table):
    B, S, H, _, D = q.shape
    nb = S // BLOCK
    qb = jnp.moveaxis(q.reshape(B, nb, BLOCK, H, 2, D), 1, 0)
    starts = jnp.arange(nb, dtype=jnp.int32) * BLOCK
    kpos = jnp.arange(S, dtype=jnp.int32)
    qoff = jnp.arange(BLOCK, dtype=jnp.int32)

    def block(args):
        qblk, start = args
        s = jnp.einsum('bqhcd,bkhcd->bhcqk', qblk, k) * (D ** -0.5)
        rel = kpos[None, :] - (start + qoff)[:, None]
        bias = jnp.transpose(bias_table[t5_bucket(rel)], (2, 0, 1))
        p = jax.nn.softmax(s.astype(jnp.float32) + bias.astype(jnp.float32)[None, :, None], axis=-1)
        a = p[:, :, 0] - lam * p[:, :, 1]
        return jnp.einsum('bhqk,bkhe->bqhe', a.astype(v.dtype), v)

    o = lax.map(block, (qb, starts))
    o = jnp.moveaxis(o, 0, 1).reshape(B, S, H, 2 * D)
    o = rms_norm(o, subln_g) * (1.0 - lam_init)
    return o.reshape(B, S, H * 2 * D)


def window_gqa(q, k, v, sink, bias_table):
    B, S, Hq, D = q.shape
    Hkv = k.shape[2]
    G = Hq // Hkv
    nb = S // BLOCK
    pad = ((0, 0), (BLOCK, BLOCK), (0, 0), (0, 0))
    kp = jnp.pad(k, pad).reshape(B, nb + 2, BLOCK, Hkv, D)
    vp = jnp.pad(v, pad).reshape(B, nb + 2, BLOCK, Hkv, D)
    kb = jnp.concatenate([kp[:, :-2], kp[:, 1:-1], kp[:, 2:]], axis=2)
    vb = jnp.concatenate([vp[:, :-2], vp[:, 1:-1], vp[:, 2:]], axis=2)
    qb = q.reshape(B, nb, BLOCK, Hkv, G, D)
    s = jnp.einsum('bnqhgd,bnkhd->bnhgqk', qb, kb) * (D ** -0.5)
    qi = jnp.arange(BLOCK, dtype=jnp.int32)
    kk = jnp.arange(3 * BLOCK, dtype=jnp.int32)
    rel = kk[None, :] - BLOCK - qi[:, None]
    bias = jnp.transpose(bias_table[t5_bucket(rel)], (2, 0, 1)).reshape(Hkv, G, BLOCK, 3 * BLOCK)
    kpos = jnp.arange(nb, dtype=jnp.int32)[:, None] * BLOCK - BLOCK + kk[None, :]
    ok = (jnp.abs(rel) <= WINDOW)[None] & ((kpos >= 0) & (kpos < S))[:, None, :]
    logits = jnp.where(ok[None, :, None, None], s.astype(jnp.float32) + bias.astype(jnp.float32)[None, None], NEG_INF)
    sink_l = jnp.broadcast_to(sink.astype(jnp.float32).reshape(Hkv, G, 1, 1), logits.shape[:-1] + (1,))
    p = jax.nn.softmax(jnp.concatenate([logits, sink_l], axis=-1), axis=-1)[..., :-1]
    o = jnp.einsum('bnhgqk,bnkhd->bnqhgd', p.astype(v.dtype), vb)
    return o.reshape(B, S, Hq * D)


def setup_inputs(seed: int = 0) -> dict:
    key = jax.random.key(seed)
    ks = jax.random.split(key, 16)

    def nrm(k, shape, scale):
        return jax.random.normal(k, shape, jnp.float32) * scale

    return {
        "x": nrm(ks[0], (BATCH, SEQ, D_MODEL), 1.0),
        "w_in": nrm(ks[1], (DEPTH, D_MODEL, W_IN_COLS), D_MODEL ** -0.5),
        "w_branch": nrm(ks[2], (DEPTH, N_BRANCHES, BRANCH_WIDTH, D_MODEL), BRANCH_WIDTH ** -0.5),
        "w_gate": nrm(ks[3], (DEPTH, D_MODEL, N_BRANCHES * D_MODEL), D_MODEL ** -0.5),
        "b_gate": nrm(ks[4], (DEPTH, N_BRANCHES * D_MODEL), 0.02),
        "w_o": nrm(ks[5], (DEPTH, D_MODEL, D_MODEL), D_MODEL ** -0.5),
        "norm_g": 1.0 + nrm(ks[6], (DEPTH, 3, D_MODEL), 0.02),
        "final_g": 1.0 + nrm(ks[7], (D_MODEL,), 0.02),
        "ffn_w_gate": nrm(ks[8], (DEPTH, 2, D_MODEL, D_FF), D_MODEL ** -0.5),
        "ffn_w_up": nrm(ks[9], (DEPTH, 2, D_MODEL, D_FF), D_MODEL ** -0.5),
        "ffn_w_down": nrm(ks[10], (DEPTH, 2, D_FF, D_MODEL), D_FF ** -0.5),
        "na_rpb": nrm(ks[11], (DEPTH, NA_HEADS, 2 * NA_WIN_ROWS - 1, 2 * NA_WIN_COLS - 1), 0.1),
        "diff_lambda": nrm(ks[12], (DEPTH, 4, HEAD_DIM), 0.1),
        "diff_subln_g": 1.0 + nrm(ks[13], (DEPTH, 2 * HEAD_DIM), 0.02),
        "gqa_sink": nrm(ks[14], (DEPTH, GQA_Q_HEADS), 0.5),
        "rel_bias_table": nrm(ks[15], (T5_BUCKETS, T5_HEADS), 0.1),
    }


def reference(x, w_in, w_branch, w_gate, b_gate, w_o, norm_g, final_g,
              ffn_w_gate, ffn_w_up, ffn_w_down, na_rpb, diff_lambda, diff_subln_g,
              gqa_sink, rel_bias_table):
    B, S, _ = x.shape
    for l in range(DEPTH):
        h = rms_norm(x, norm_g[l, 0])
        x = x + 0.5 * swiglu(h, ffn_w_gate[l, 0], ffn_w_up[l, 0], ffn_w_down[l, 0])

        h = rms_norm(x, norm_g[l, 1])
        proj = h @ w_in[l]
        qa, ka, va, qd, kd, vd, qc, kc, vc = jnp.split(proj, PROJ_OFFSETS, axis=-1)

        ya = neighbourhood_attention(qa.reshape(B, S, NA_HEADS, HEAD_DIM),
                                     ka.reshape(B, S, NA_HEADS, HEAD_DIM),
                                     va.reshape(B, S, NA_HEADS, HEAD_DIM), na_rpb[l])

        lam_init = 0.8 - 0.6 * math.exp(-0.3 * l)
        lq = diff_lambda[l].astype(jnp.float32)
        lam = jnp.exp(jnp.sum(lq[0] * lq[1])) - jnp.exp(jnp.sum(lq[2] * lq[3])) + lam_init
        yb = diff_attention(qd.reshape(B, S, DIFF_HEADS, 2, HEAD_DIM),
                            kd.reshape(B, S, DIFF_HEADS, 2, HEAD_DIM),
                            vd.reshape(B, S, DIFF_HEADS, 2 * HEAD_DIM),
                            lam, lam_init, diff_subln_g[l], rel_bias_table[:, :DIFF_HEADS])

        yc = window_gqa(qc.reshape(B, S, GQA_Q_HEADS, HEAD_DIM),
                        kc.reshape(B, S, GQA_KV_HEADS, HEAD_DIM),
                        vc.reshape(B, S, GQA_KV_HEADS, HEAD_DIM),
                        gqa_sink[l], rel_bias_table[:, DIFF_HEADS:])

        g = jax.nn.sigmoid(h @ w_gate[l] + b_gate[l]).reshape(B, S, N_BRANCHES, D_MODEL)
        merged = (g[:, :, 0] * (ya @ w_branch[l, 0])
                  + g[:, :, 1] * (yb @ w_branch[l, 1])
                  + g[:, :, 2] * (yc @ w_branch[l, 2]))
        x = x + merged @ w_o[l]

        h = rms_norm(x, norm_g[l, 2])
        x = x + 0.5 * swiglu(h, ffn_w_gate[l, 1], ffn_w_up[l, 1], ffn_w_down[l, 1])
    return rms_norm(x, final_g)
```

```python
import math
import numpy as np
import ml_dtypes
import concourse.bass as bass
import concourse.mybir as mybir
from concourse.bass_utils import run_bass_kernel_spmd

F32 = mybir.dt.float32
BF16 = mybir.dt.bfloat16
AF = mybir.ActivationFunctionType
ALU = mybir.AluOpType

D = 1024
DFF = 2816
NT = 16
TOK = NT * 128
EPS = 1e-6
DEPTH = 4
NCORES = 8
NEG = -30000.0
WIN = 3840
QA, KA, VA, QD, KD, VD, QC, KC, VC = 0, 512, 1024, 1536, 2048, 2560, 3072, 3584, 3712

ENGS = ("pe", "act", "dve", "pool", "sp")
EPOCH = 24000


class Slot:
    def __init__(self, sem):
        self.sem = sem
        self.count = 0


class Prog:
    def __init__(self, nc):
        self.nc = nc
        self.ops = {e: [] for e in ENGS}
        self.cnt = {e: 0 for e in ENGS}
        self.esem = {}
        self.waited = {e: {} for e in ENGS}
        self.waited_eng = {e: {} for e in ENGS}
        self.last_w = {}
        self.readers = {}
        self.nsem = 0

    def new_sem(self, name):
        self.nsem += 1
        return self.nc.alloc_semaphore(name)

    def _eng_sem(self, eng, epoch):
        k = (eng, epoch)
        if k not in self.esem:
            self.esem[k] = self.new_sem(f"e_{eng}_{epoch}")
        return self.esem[k]

    def _emit_wait(self, consumer, ev):
        if ev is None:
            return
        if ev[0] == 'eng':
            _, eng, n = ev
            if eng == consumer and eng == 'pe':
                return
            prev = self.waited_eng[consumer].get(eng, 0)
            if prev >= n:
                return
            self.waited_eng[consumer][eng] = n
            epoch = (n - 1) // EPOCH
            val = n - epoch * EPOCH
            sem = self._eng_sem(eng, epoch)
            self.ops[consumer].append(lambda e, sem=sem, val=val: e.wait_ge(sem, val))
        else:
            _, slot, count = ev
            key = id(slot)
            prev = self.waited[consumer].get(key, 0)
            if prev >= count:
                return
            self.waited[consumer][key] = count
            self.ops[consumer].append(lambda e, sem=slot.sem, val=count: e.wait_ge(sem, val))

    def _deps(self, consumer, reads, writes):
        for k in reads:
            self._emit_wait(consumer, self.last_w.get(k))
        for k in writes:
            self._emit_wait(consumer, self.last_w.get(k))
            for ev in self.readers.get(k, ()):
                if ev[0] == 'eng' and ev[1] == consumer:
                    continue
                self._emit_wait(consumer, ev)

    def _record(self, ev, reads, writes):
        for k in reads:
            lst = self.readers.setdefault(k, [])
            lst.append(ev)
            if len(lst) > 12:
                comp = {}
                for e2 in lst:
                    kk = (e2[0], e2[1] if e2[0] == 'eng' else id(e2[1]))
                    if kk not in comp or comp[kk][2] < e2[2]:
                        comp[kk] = e2
                self.readers[k] = list(comp.values())
        for k in writes:
            self.last_w[k] = ev
            self.readers[k] = []

    def op(self, eng, fn, reads=(), writes=()):
        self._deps(eng, reads, writes)
        self.cnt[eng] += 1
        n = self.cnt[eng]
        sem = self._eng_sem(eng, (n - 1) // EPOCH)
        self.ops[eng].append(lambda e, fn=fn, sem=sem: fn(e).then_inc(sem, 1))
        self._record(('eng', eng, n), reads, writes)

    def dma(self, queue, out, in_, reads=(), writes=(), slot=None):
        if slot is None:
            k = ("slot", writes[0] if writes else ("rd", reads[0]))
            if k not in self.esem:
                self.esem[k] = Slot(self.new_sem("d_" + "_".join(str(t) for t in k[1])))
            slot = self.esem[k]
        if slot.count:
            self._emit_wait(queue, ('dma', slot, slot.count))
        self._deps(queue, reads, writes)
        slot.count += 16
        self.ops[queue].append(
            lambda e, out=out, in_=in_, sem=slot.sem: e.dma_start(out=out, in_=in_).then_inc(sem, 16))
        ev = ('dma', slot, slot.count)
        self._record(ev, reads, writes)
        return ev

    def barrier(self):
        slots = [v for v in self.esem.values() if isinstance(v, Slot)]
        for c in ENGS:
            for p in ENGS:
                if self.cnt[p] and not (p == c and p in ("pe", "sp")):
                    self._emit_wait(c, ('eng', p, self.cnt[p]))
            for sl in slots:
                if sl.count:
                    self._emit_wait(c, ('dma', sl, sl.count))

    def emit(self):
        nc, ops = self.nc, self.ops
        with nc.Block() as block:
            @block.tensor
            def _(e):
                for f in ops["pe"]:
                    f(e)

            @block.scalar
            def _(e):
                for f in ops["act"]:
                    f(e)

            @block.vector
            def _(e):
                for f in ops["dve"]:
                    f(e)

            @block.gpsimd
            def _(e):
                for f in ops["pool"]:
                    f(e)

            @block.sync
            def _(e):
                for f in ops["sp"]:
                    f(e)


class KB:
    SB_LO = 16640
    SB_HI = 229376

    def __init__(self):
        self.nc = bass.Bass("TRN2", target_bir_lowering=False)
        self.P = Prog(self.nc)
        self.top = self.SB_LO
        self.nname = 0
        self.pb = self.nc.alloc_psum_tensor("pb", [128, 8, 512], F32)
        self.cnt = {}

    def mark(self):
        return self.top

    def release(self, m):
        self.top = m
        self.P.barrier()

    def sb(self, shape, dt):
        n = 1
        for s in shape[1:]:
            n *= s
        nbytes = n * (4 if dt == F32 else 2)
        nbytes = (nbytes + 63) // 64 * 64
        off = self.top
        self.top += nbytes
        assert self.top <= self.SB_HI, f"SBUF overflow {self.top}"
        self.nname += 1
        return self.nc.alloc_sbuf_tensor_at(f"t{self.nname}", list(shape), dt, offset=off)

    def din(self, name, shape, dt=F32):
        return self.nc.dram_tensor(name, list(shape), dt, kind="ExternalInput").ap()

    def dout(self, name, shape, dt=F32):
        return self.nc.dram_tensor(name, list(shape), dt, kind="ExternalOutput").ap()

    def rr(self, name, n):
        c = self.cnt.get(name, 0)
        self.cnt[name] = c + 1
        return c % n

    def mm(self, out, lhsT, rhs, start, stop, reads, writes, skip=False):
        if skip:
            self.P.op("pe", lambda e: e.matmul(out, lhsT=lhsT, rhs=rhs, start=start, stop=stop, skip_group_check=True),
                      reads, writes)
        else:
            self.P.op("pe", lambda e: e.matmul(out, lhsT=lhsT, rhs=rhs, start=start, stop=stop), reads, writes)

    def act(self, out, in_, func, reads, writes, bias=None, scale=None, accum_out=None):
        kw = {}
        if bias is not None:
            kw["bias"] = bias
        if scale is not None:
            kw["scale"] = scale
        if accum_out is not None:
            kw["accum_out"] = accum_out
        self.P.op("act", lambda e: e.activation(out=out, in_=in_, func=func, **kw), reads, writes)

    def ts(self, eng, out, in0, s1, s2, op0, op1, reads, writes):
        if op1 is None:
            self.P.op(eng, lambda e: e.tensor_scalar(out=out, in0=in0, scalar1=s1, scalar2=None, op0=op0), reads, writes)
        else:
            self.P.op(eng, lambda e: e.tensor_scalar(out=out, in0=in0, scalar1=s1, scalar2=s2, op0=op0, op1=op1),
                      reads, writes)

    def tt(self, eng, out, in0, in1, op, reads, writes):
        self.P.op(eng, lambda e: e.tensor_tensor(out=out, in0=in0, in1=in1, op=op), reads, writes)

    def stt(self, eng, out, in0, scalar, in1, op0, op1, reads, writes):
        self.P.op(eng, lambda e: e.scalar_tensor_tensor(out=out, in0=in0, scalar=scalar, in1=in1, op0=op0, op1=op1),
                  reads, writes)

    def copy(self, eng, out, in_, reads, writes):
        if eng == "act":
            self.P.op("act", lambda e: e.copy(out=out, in_=in_), reads, writes)
        else:
            self.P.op(eng, lambda e: e.tensor_copy(out=out, in_=in_), reads, writes)

    def recip(self, out, in_, reads, writes):
        self.P.op("dve", lambda e: e.reciprocal(out=out, in_=in_), reads, writes)

    def memset(self, eng, ap, val, writes):
        self.P.op(eng, lambda e: e.memset(ap, val), (), writes)

    def dma(self, q, out, in_, reads=(), writes=(), slot=None):
        return self.P.dma(q, out, in_, reads, writes, slot)


XKEYS = [("x", tt, dh) for tt in range(NT) for dh in range(2)]


def setup_base(K, x_d, ident_d):
    C = {}
    C["x"] = K.sb([128, NT, D], F32)
    C["hT"] = K.sb([128, 8, TOK], BF16)
    C["ssq"] = K.sb([128, NT], F32)
    C["rstd"] = K.sb([128, NT], F32)
    C["xn"] = K.sb([128, 2, D], BF16)
    C["junk"] = K.sb([128, D], BF16)
    C["ident"] = K.sb([128, 128], BF16)
    C["gbc"] = K.sb([128, D], F32)
    K.dma("sp", C["x"][:], x_d.rearrange("(tt p) d -> p tt d", p=128), writes=XKEYS)
    K.dma("pool", C["ident"][:], ident_d, writes=[("ident",)])
    return C


def emit_norm_T(K, C, g_row_d):
    x_sb, hT, ssq, rstd, xn, junk, ident, gbc = (C[k] for k in ("x", "hT", "ssq", "rstd", "xn", "junk", "ident", "gbc"))
    pb = K.pb
    K.dma("sp", gbc[:], g_row_d.broadcast_to([128, D]), writes=[("gbc",)])
    for tt in range(NT):
        xk = [("x", tt, 0), ("x", tt, 1)]
        K.act(junk[:], x_sb[:, tt, :], AF.Square, xk, [("junk",), ("ssq", tt)], accum_out=ssq[:, tt:tt + 1])
        K.ts("dve", rstd[:, tt:tt + 1], ssq[:, tt:tt + 1], 1.0 / D, EPS, ALU.mult, ALU.add, [("ssq", tt)], [("rstd", tt)])
        K.P.op("act", lambda e, tt=tt: e.sqrt(out=rstd[:, tt:tt + 1], in_=rstd[:, tt:tt + 1]), [("rstd", tt)], [("rstd", tt)])
        K.recip(rstd[:, tt:tt + 1], rstd[:, tt:tt + 1], [("rstd", tt)], [("rstd", tt)])
        s = tt % 2
        K.stt("dve", xn[:, s, :], x_sb[:, tt, :], rstd[:, tt:tt + 1], gbc[:], ALU.mult, ALU.mult,
              xk + [("rstd", tt), ("gbc",)], [("xn", s)])
        for kc in range(8):
            K.mm(pb[:, 6 + kc // 4, (kc % 4) * 128:(kc % 4 + 1) * 128], xn[:, s, kc * 128:(kc + 1) * 128], ident[:],
                 True, True, [("xn", s), ("ident",)], [("pb", 6 + kc // 4)])
        K.copy("act", hT[:, :, tt * 128:(tt + 1) * 128], pb[:, 6:8, :].rearrange("p b (k t) -> p (b k) t", t=128),
               [("pb", 6), ("pb", 7)], [("hT", tt)])


def emit_ffn(K, C, wg_d, wu_d, wd_d):
    x_sb, hT, pb = C["x"], C["hT"], K.pb
    m = K.mark()
    wg_sb = K.sb([128, 2, 8, 256], BF16)
    wu_sb = K.sb([128, 2, 8, 256], BF16)
    wd_sb = K.sb([128, 2, 2, D], BF16)
    sg = K.sb([128, 2, 512], F32)
    uT = K.sb([128, 2, 2, 512], BF16)
    FG = 256
    wg_v = wg_d.rearrange("(kc p) f -> p kc f", p=128)
    wu_v = wu_d.rearrange("(kc p) f -> p kc f", p=128)
    wd_v = wd_d.rearrange("(fc p) d -> p fc d", p=128)
    items = []
    wslot = {}

    def gateup(fg, tg):
        if tg == 0:
            ws = K.rr("ffn_w", 2)
            wslot[fg] = ws
            f0 = fg * FG
            K.dma("pool", wg_sb[:, ws, :, :], wg_v[:, :, f0:f0 + FG], writes=[("wg", ws)])
            K.dma("pool", wu_sb[:, ws, :, :], wu_v[:, :, f0:f0 + FG], writes=[("wu", ws)])
            K.dma("pool", wd_sb[:, ws, :, :], wd_v[:, 2 * fg:2 * fg + 2, :], writes=[("wd", ws)])
        ws = wslot[fg]
        us = K.rr("ffn_u", 2)
        hk = [("hT", tt) for tt in range(tg * 4, tg * 4 + 4)]
        for fc in range(2):
            b = K.rr("ffn_b", 2)
            for kc in range(8):
                K.mm(pb[:, b, :], wg_sb[:, ws, kc, fc * 128:(fc + 1) * 128], hT[:, kc, tg * 512:(tg + 1) * 512],
                     kc == 0, kc == 7, hk + [("wg", ws)], [("pb", b)])
            for kc in range(8):
                K.mm(pb[:, 2 + b, :], wu_sb[:, ws, kc, fc * 128:(fc + 1) * 128], hT[:, kc, tg * 512:(tg + 1) * 512],
                     kc == 0, kc == 7, hk + [("wu", ws)], [("pb", 2 + b)])
            K.act(sg[:, b, :], pb[:, b, :], AF.Silu, [("pb", b)], [("sg", b)])
            K.tt("dve", uT[:, us, fc, :], sg[:, b, :], pb[:, 2 + b, :], ALU.mult, [("sg", b), ("pb", 2 + b)],
                 [("uT", us, fc)])
        return us

    def down(fg, tg, us):
        ws = wslot[fg]
        for t4 in range(4):
            tt = tg * 4 + t4
            for dh in range(2):
                yb = 4 + K.rr("ffn_y", 2)
                for fc in range(2):
                    K.mm(pb[:, yb, :], uT[:, us, fc, t4 * 128:(t4 + 1) * 128], wd_sb[:, ws, fc, dh * 512:(dh + 1) * 512],
                         fc == 0, fc == 1, [("uT", us, fc), ("wd", ws)], [("pb", yb)])
                K.stt("dve", x_sb[:, tt, dh * 512:(dh + 1) * 512], pb[:, yb, :], 0.5,
                      x_sb[:, tt, dh * 512:(dh + 1) * 512], ALU.mult, ALU.add,
                      [("pb", yb), ("x", tt, dh)], [("x", tt, dh)])

    seq = [(fg, tg) for fg in range(DFF // FG) for tg in range(4)]
    prev = None
    for (fg, tg) in seq:
        us = gateup(fg, tg)
        if prev is not None:
            down(*prev)
        prev = (fg, tg, us)
    down(*prev)
    K.release(m)


def emit_proj_T(K, C, w_cols_d, ncols_chunks, dst_fn, scale, tagw):
    hT, pb = C["hT"], K.pb
    m = K.mark()
    wsb = K.sb([128, 2, 8, 128], BF16)
    for ci in range(ncols_chunks):
        ws = K.rr("pw" + tagw, 2)
        src = w_cols_d(ci)
        if isinstance(src, tuple):
            for hh, s_ in enumerate(src):
                K.dma("pool", wsb[:, ws, :, hh * 64:(hh + 1) * 64], s_.rearrange("(kc p) c -> p kc c", p=128),
                      writes=[("pw", tagw, ws)] if hh == 0 else [("pw2", tagw, ws)])
            wk = [("pw", tagw, ws), ("pw2", tagw, ws)]
        else:
            K.dma("pool", wsb[:, ws, :, :], src.rearrange("(kc p) c -> p kc c", p=128),
                  writes=[("pw", tagw, ws), ("pw2", tagw, ws)])
            wk = [("pw", tagw, ws)]
        for tg in range(4):
            b = K.rr("pj_b", 2)
            hk = [("hT", tt) for tt in range(tg * 4, tg * 4 + 4)]
            for kc in range(8):
                K.mm(pb[:, b, :], wsb[:, ws, kc, :], hT[:, kc, tg * 512:(tg + 1) * 512], kc == 0, kc == 7,
                     hk + wk, [("pb", b)])
            dst, keys = dst_fn(ci, tg)
            if scale is None:
                K.copy("act" if tg % 2 == 0 else "dve", dst, pb[:, b, :], [("pb", b)], keys)
            else:
                K.ts("dve", dst, pb[:, b, :], scale, None, ALU.mult, None, [("pb", b)], keys) if tg % 2 else \
                    K.P.op("act", lambda e, dst=dst, b=b: e.mul(out=dst, in_=pb[:, b, :], mul=scale), [("pb", b)], keys)
    K.release(m)


def emit_kv(K, C, w_in_d, kT_out, vA_out, vD_out, vC_out):
    hT, pb = C["hT"], K.pb
    m = K.mark()
    kst = K.sb([128, 2, TOK], BF16)
    kcols = [KA + 128 * i for i in range(4)] + [KD + 128 * i for i in range(4)] + [KC]

    def dst_fn(ci, tg):
        s = ci % 2
        return kst[:, s, tg * 512:(tg + 1) * 512], [("kst", s, tg)]

    hT_ = hT
    wsb = K.sb([128, 2, 8, 128], BF16)
    for ci in range(9):
        ws = K.rr("pwk", 2)
        K.dma("pool", wsb[:, ws, :, :], w_in_d[:, kcols[ci]:kcols[ci] + 128].rearrange("(kc p) c -> p kc c", p=128),
              writes=[("pwk", ws)])
        s = ci % 2
        for tg in range(4):
            b = K.rr("pj_b", 2)
            hk = [("hT", tt) for tt in range(tg * 4, tg * 4 + 4)]
            for kc in range(8):
                K.mm(pb[:, b, :], wsb[:, ws, kc, :], hT_[:, kc, tg * 512:(tg + 1) * 512], kc == 0, kc == 7,
                     hk + [("pwk", ws)], [("pb", b)])
            K.copy("act" if tg % 2 == 0 else "dve", kst[:, s, tg * 512:(tg + 1) * 512], pb[:, b, :], [("pb", b)],
                   [("kst", s, tg)])
        K.dma("sp", kT_out[ci], kst[:, s, :], reads=[("kst", s, tg) for tg in range(4)], slot=K.kvslot(("kst", s)))
    wv = K.sb([128, 8, 1152], BF16)
    w_v = w_in_d.rearrange("(kc p) c -> p kc c", p=128)
    K.dma("pool", wv[:, :, 0:512], w_v[:, :, VA:VA + 512], writes=[("wv", 0)])
    K.dma("pool", wv[:, :, 512:1024], w_v[:, :, VD:VD + 512], writes=[("wv", 1)])
    K.dma("pool", wv[:, :, 1024:1152], w_v[:, :, VC:VC + 128], writes=[("wv", 2)])
    vA_st = K.sb([128, 8, NT, 65], BF16)
    vD_st = K.sb([128, 4, NT, 129], BF16)
    vC_st = K.sb([128, 2, NT, 65], BF16)
    K.memset("pool", vA_st[:, :, :, 64:65], 1.0, [("vAones",)])
    K.memset("pool", vD_st[:, :, :, 128:129], 1.0, [("vDones",)])
    K.memset("pool", vC_st[:, :, :, 64:65], 1.0, [("vCones",)])
    for tt in range(NT):
        for (j, c0, nc_, st, nh, hd, key) in ((0, 0, 512, vA_st, 8, 64, "vA"), (1, 512, 512, vD_st, 4, 128, "vD"),
                                               (2, 1024, 128, vC_st, 2, 64, "vC")):
            b = 2 + K.rr("v_b", 3)
            for kc in range(8):
                K.mm(pb[:, b, 0:nc_], hT[:, kc, tt * 128:(tt + 1) * 128], wv[:, kc, c0:c0 + nc_], kc == 0, kc == 7,
                     [("hT", tt), ("wv", j)], [("pb", b)])
            K.copy("act" if (tt + j) % 2 else "dve", st[:, :, tt, 0:hd],
                   pb[:, b, 0:nc_].rearrange("p (h e) -> p h e", h=nh), [("pb", b)], [(key, tt)])
    for (st, outd, key, ones) in ((vA_st, vA_out, "vA", "vAones"), (vD_st, vD_out, "vD", "vDones"), (vC_st, vC_out, "vC", "vCones")):
        K.dma("sp", outd.rearrange("h p t e -> p h t e"), st[:], reads=[(key, tt) for tt in range(NT)] + [(ones,)],
              slot=K.kvslot((key,)))
    K.release(m)


def emit_yT(K, C, ytmp, ykey, yT, chunk, jl):
    pb, ident = K.pb, C["ident"]
    K.mm(pb[:, 7, 0:128], ytmp, ident[:], True, True, [ykey, ("ident",)], [("pb", 7)])
    K.copy("dve", yT[:, chunk, jl * 128:(jl + 1) * 128], pb[:, 7, 0:128], [("pb", 7)], [("yT", chunk, jl)])


def emit_attn_diff(K, C, qT, yT, kT_all, vD_all, biasD_d, fb, lamneg, gsubcol, NC):
    pb, ident = K.pb, C["ident"]
    m = K.mark()
    kD = K.sb([128, 2, TOK], BF16)
    vD = K.sb([128, 2, NT, 129], BF16)
    PT = K.sb([128, 3, 2, 512], BF16)
    bmat = K.sb([128, 8, 512], BF16)
    Ps = K.sb([128, 2, 512], F32)
    ones = K.sb([128, 128], F32)
    rden = K.sb([128, 2, 512], F32)
    at = K.sb([128, 2, 512], F32)
    rs = K.sb([128, 512], F32)
    K.memset("dve", ones[:], 1.0, [("ones",)])
    for h in range(4):
        K.dma("pool", bmat[:], biasD_d[h].rearrange("m p q -> p m q"), writes=[("bmat",)])
        for T in range(4):
            qk = [("qT", h, T)]
            blocks = [(i, b) for i in range(NC) for b in range(NT)]
            slot_of = {}

            def load_shard(i):
                s_ = K.rr("dkv", 2)
                slot_of[i] = s_
                K.dma("sp", kD[:, s_, :], kT_all[i, 4 + h], writes=[("kD", s_)])
                K.dma("sp", vD[:, s_, :, :], vD_all[i, h], writes=[("vD", s_)])

            def classify(i, b):
                mat = None
                if i == 0:
                    o = b - 4 * T
                    if -1 <= o <= 4:
                        mat, bias = o + 1, 0.0
                    elif o < -1:
                        bias = fb[:, h, 0:1]
                    else:
                        bias = fb[:, h, 1:2]
                elif i == NC - 1 and b == NT - 1 and T == 0:
                    mat, bias = 6, 0.0
                elif i == 1 and b == 0 and T == 3:
                    mat, bias = 7, 0.0
                else:
                    bias = fb[:, h, 2 + i:3 + i]
                return mat, bias

            def scores(n):
                i, b = blocks[n]
                if b == 0:
                    load_shard(i)
                s_ = slot_of[i]
                mat, bias = classify(i, b)
                u = n % 3
                for c in range(2):
                    bank = 2 * u + c
                    K.mm(pb[:, bank, :], kD[c * 64:(c + 1) * 64, s_, b * 128:(b + 1) * 128],
                         qT[c * 64:(c + 1) * 64, h, T * 512:(T + 1) * 512], True, mat is None,
                         [("kD", s_)] + qk, [("pb", bank)])
                    if mat is not None:
                        K.mm(pb[:, bank, :], ident[:], bmat[:, mat, :], False, True, [("ident",), ("bmat",)],
                             [("pb", bank)])
                rd = [("pb", 2 * u), ("pb", 2 * u + 1)] + ([] if isinstance(bias, float) else [("fb",)])
                K.act(PT[:, u, :, :], pb[:, 2 * u:2 * u + 2, :], AF.Exp, rd, [("PT", u, 0), ("PT", u, 1)], bias=bias)

            def values(n):
                i, b = blocks[n]
                s_ = slot_of[i]
                u = n % 3
                first, last = (n == 0), (n == len(blocks) - 1)
                for c in range(2):
                    K.mm(pb[:, 6 + c, :], vD[:, s_, b, 0:128], PT[:, u, c, :], first, last,
                         [("PT", u, c), ("vD", s_)], [("pb", 6 + c)])
                    eng = "dve" if c == 0 else "pool"
                    if first:
                        K.copy(eng, Ps[:, c, :], PT[:, u, c, :], [("PT", u, c)], [("Ps", c)])
                    else:
                        K.tt(eng, Ps[:, c, :], Ps[:, c, :], PT[:, u, c, :], ALU.add, [("Ps", c), ("PT", u, c)], [("Ps", c)])

            scores(0)
            scores(1)
            for n in range(len(blocks)):
                if n + 2 < len(blocks):
                    scores(n + 2)
                values(n)
            for c in range(2):
                K.mm(pb[:, c, :], ones[:], Ps[:, c, :], True, True, [("ones",), ("Ps", c)], [("pb", c)])
                K.recip(rden[:, c, :], pb[:, c, :], [("pb", c)], [("rden", c)])
            K.tt("dve", at[:, 0, :], pb[:, 6, :], rden[:, 0, :], ALU.mult, [("pb", 6), ("rden", 0)], [("at", 0)])
            K.tt("dve", rden[:, 1, :], pb[:, 7, :], rden[:, 1, :], ALU.mult, [("pb", 7), ("rden", 1)], [("rden", 1)])
            K.stt("dve", at[:, 0, :], rden[:, 1, :], lamneg[:, 0:1], at[:, 0, :], ALU.mult, ALU.add,
                  [("rden", 1), ("at", 0), ("lam",)], [("at", 0)])
            K.act(at[:, 1, :], at[:, 0, :], AF.Square, [("at", 0)], [("at", 1)])
            K.mm(pb[:, 2, :], ones[:], at[:, 1, :], True, True, [("ones",), ("at", 1)], [("pb", 2)])
            K.ts("dve", rs[:], pb[:, 2, :], 1.0 / 128, EPS, ALU.mult, ALU.add, [("pb", 2)], [("rs",)])
            K.P.op("act", lambda e: e.sqrt(out=rs[:], in_=rs[:]), [("rs",)], [("rs",)])
            K.recip(rs[:], rs[:], [("rs",)], [("rs",)])
            K.stt("dve", yT[:, h, T * 512:(T + 1) * 512], at[:, 0, :], gsubcol[:, 0:1], rs[:], ALU.mult, ALU.mult,
                  [("at", 0), ("rs",), ("gsub",)], [("yT", h, jl) for jl in range(T * 4, T * 4 + 4)])
    K.release(m)


def emit_attn_A(K, C, qT, yT, kT_all, vA_all, biasA_d, maskA_d, NC):
    pb, ident = K.pb, C["ident"]
    m = K.mark()
    kA = K.sb([128, 2, 22 * 128], BF16)
    vA = K.sb([128, 2, 2, 22, 65], BF16)
    bA = K.sb([128, 2, 896], BF16)
    mA = K.sb([128, 5, 896], BF16)
    PT = K.sb([128, 2, 896], BF16)
    rc = K.sb([128, 2], F32)
    ytmp = K.sb([128, 2, 128], BF16)
    K.dma("pool", mA[:], maskA_d.rearrange("s p q -> p s q"), writes=[("mA",)])
    prev, nxt = NC - 1, (1 if NC > 1 else 0)
    for ch in range(4):
        s = K.rr("akv", 2)
        K.dma("sp", kA[:, s, 384:384 + TOK], kT_all[0, ch], writes=[("kA", s, 0)])
        K.dma("sp", kA[:, s, 0:384], kT_all[prev, ch][:, 13 * 128:16 * 128], writes=[("kA", s, 1)])
        K.dma("sp", kA[:, s, 384 + TOK:768 + TOK], kT_all[nxt, ch][:, 0:384], writes=[("kA", s, 2)])
        for hh in range(2):
            K.dma("sp", vA[:, s, hh, 3:19, :], vA_all[0, 2 * ch + hh], writes=[("vA", s, hh, 0)])
            K.dma("sp", vA[:, s, hh, 0:3, :], vA_all[prev, 2 * ch + hh][:, 13:16, :], writes=[("vA", s, hh, 1)])
            K.dma("sp", vA[:, s, hh, 19:22, :], vA_all[nxt, 2 * ch + hh][:, 0:3, :], writes=[("vA", s, hh, 2)])
        kk = [("kA", s, t) for t in range(3)]
        for hh in range(2):
            h = 2 * ch + hh
            K.dma("pool", bA[:, hh, :], biasA_d[h], writes=[("bA", hh)])
        itemsA = [(jl, hh) for jl in range(NT) for hh in range(2)]

        def scoresA(n):
            jl, hh = itemsA[n]
            mset = {0: 1, 1: 2, NT - 2: 3, NT - 1: 4}.get(jl, 0)
            pr = hh * 64
            u = n % 2
            b0 = 2 * u
            for oi in range(7):
                bank = b0 + oi // 4
                K.mm(pb[:, bank, (oi % 4) * 128:(oi % 4 + 1) * 128],
                     kA[pr:pr + 64, s, (jl + oi) * 128:(jl + oi + 1) * 128],
                     qT[pr:pr + 64, ch, jl * 128:(jl + 1) * 128], oi % 4 == 0, False, kk + [("qT", ch, jl // 4)],
                     [("pb", bank)], skip=True)
            for half, (c0, c1) in enumerate(((0, 512), (512, 896))):
                bank = b0 + half
                K.mm(pb[:, bank, 0:c1 - c0], ident[:], bA[:, hh, c0:c1], False, False, [("ident",), ("bA", hh)],
                     [("pb", bank)], skip=True)
                K.mm(pb[:, bank, 0:c1 - c0], ident[:], mA[:, mset, c0:c1], False, True, [("ident",), ("mA",)],
                     [("pb", bank)], skip=True)
                K.act(PT[:, u, c0:c1], pb[:, bank, 0:c1 - c0], AF.Exp, [("pb", bank)], [("PTA", u, half)])

        def valuesA(n):
            jl, hh = itemsA[n]
            u = n % 2
            yt = jl % 2
            a = K.rr("aacc", 2)
            accv = pb[:, 4 + a, 0:65]
            for oi in range(7):
                K.mm(accv, PT[:, u, oi * 128:(oi + 1) * 128], vA[:, s, hh, jl + oi, :], oi == 0, oi == 6,
                     [("PTA", u, oi // 4)] + [("vA", s, hh, t) for t in range(3)], [("pb", 4 + a)])
            K.recip(rc[:, a:a + 1], accv[:, 64:65], [("pb", 4 + a)], [("rcA", a)])
            K.ts("dve", ytmp[:, yt, hh * 64:(hh + 1) * 64], accv[:, 0:64], rc[:, a:a + 1], None, ALU.mult, None,
                 [("pb", 4 + a), ("rcA", a)], [("aytmp", yt)])
            if hh == 1:
                emit_yT(K, C, ytmp[:, yt, :], ("aytmp", yt), yT, ch, jl)

        scoresA(0)
        for n in range(len(itemsA)):
            if n + 1 < len(itemsA):
                scoresA(n + 1)
            valuesA(n)
    K.release(m)


def emit_attn_C(K, C, qT, yT, kT_all, vC_all, biasC_d, maskC_d, sinkexp, NC):
    pb, ident = K.pb, C["ident"]
    m = K.mark()
    kC = K.sb([128, 18 * 128], BF16)
    vC = K.sb([128, 2, 18, 65], BF16)
    bC = K.sb([128, 6, 512], BF16)
    mC = K.sb([128, 9, 512], BF16)
    PT = K.sb([128, 3, 512], BF16)
    den = K.sb([128, 8], F32)
    ytmp = K.sb([128, 2, 256], BF16)
    prev, nxt = NC - 1, (1 if NC > 1 else 0)
    K.dma("pool", bC[:], biasC_d.rearrange("m p q -> p m q"), writes=[("bC",)])
    K.dma("pool", mC[:], maskC_d.rearrange("m p q -> p m q"), writes=[("mC",)])
    K.dma("sp", kC[:, 128:128 + TOK], kT_all[0, 8], writes=[("kC", 0)])
    K.dma("sp", kC[:, 0:128], kT_all[prev, 8][:, 15 * 128:16 * 128], writes=[("kC", 1)])
    K.dma("sp", kC[:, 128 + TOK:256 + TOK], kT_all[nxt, 8][:, 0:128], writes=[("kC", 2)])
    for g in range(2):
        K.dma("sp", vC[:, g, 1:17, :], vC_all[0, g], writes=[("vC", g, 0)])
        K.dma("sp", vC[:, g, 0:1, :], vC_all[prev, g][:, 15:16, :], writes=[("vC", g, 1)])
        K.dma("sp", vC[:, g, 17:18, :], vC_all[nxt, g][:, 0:1, :], writes=[("vC", g, 2)])
    kk = [("kC", t) for t in range(3)]
    itemsC = [(jl, g, oi) for jl in range(NT) for g in range(2) for oi in range(3)]

    def scoresC(n):
        jl, g, oi = itemsC[n]
        mset = 1 if jl == 0 else (2 if jl == NT - 1 else 0)
        pr = g * 64
        u = n % 3
        K.mm(pb[:, u, :], kC[pr:pr + 64, (jl + oi) * 128:(jl + oi + 1) * 128],
             qT[pr:pr + 64, :, jl * 128:(jl + 1) * 128], True, False, kk + [("qT", c_, jl // 4) for c_ in range(4)],
             [("pb", u)])
        K.mm(pb[:, u, :], ident[:], bC[:, g * 3 + oi, :], False, False, [("ident",), ("bC",)], [("pb", u)])
        K.mm(pb[:, u, :], ident[:], mC[:, mset * 3 + oi, :], False, True, [("ident",), ("mC",)], [("pb", u)])
        K.act(PT[:, u, :], pb[:, u, :], AF.Exp, [("pb", u)], [("PTC", u)])

    def valuesC(n):
        jl, g, oi = itemsC[n]
        u = n % 3
        a = (jl * 2 + g) % 2
        yt = a
        for hq in range(4):
            K.mm(pb[:, 4 + a, hq * 128:hq * 128 + 65], PT[:, u, hq * 128:(hq + 1) * 128], vC[:, g, jl + oi, :],
                 oi == 0 and hq == 0, oi == 2, [("PTC", u)] + [("vC", g, t) for t in range(3)], [("pb", 4 + a)],
                 skip=True)
        if oi == 2:
            for hq in range(4):
                hg = g * 4 + hq
                accv = pb[:, 4 + a, hq * 128:hq * 128 + 65]
                K.tt("dve", den[:, hg:hg + 1], accv[:, 64:65], sinkexp[:, hg:hg + 1], ALU.add,
                     [("pb", 4 + a), ("sinkexp",)], [("den", hg)])
                K.recip(den[:, hg:hg + 1], den[:, hg:hg + 1], [("den", hg)], [("den", hg)])
                K.ts("dve", ytmp[:, yt, hq * 64:(hq + 1) * 64], accv[:, 0:64], den[:, hg:hg + 1], None, ALU.mult, None,
                     [("pb", 4 + a), ("den", hg)], [("cytmp", yt)])
            for half in range(2):
                emit_yT(K, C, ytmp[:, yt, half * 128:(half + 1) * 128], ("cytmp", yt), yT, g * 2 + half, jl)

    scoresC(0)
    for n in range(len(itemsC)):
        if n + 1 < len(itemsC):
            scoresC(n + 1)
        valuesC(n)
    K.release(m)


def emit_branch(K, C, yT, i, w_gate_d, bgate, w_branch_d, w_o_d):
    x_sb, hT, pb = C["x"], C["hT"], K.pb
    m = K.mark()
    wg = K.sb([128, 8, D], BF16)
    wb = K.sb([128, 4, D], BF16)
    wo = K.sb([128, 8, D], BF16)
    sig = K.sb([128, 2, 512], F32)
    mT = K.sb([128, 8, 512], BF16)
    wgv = w_gate_d.rearrange("(kc p) c -> p kc c", p=128)
    wbv = w_branch_d[i].rearrange("(ec p) c -> p ec c", p=128)
    for oc in range(8):
        K.dma("pool", wg[:, :, oc * 128:(oc + 1) * 128], wgv[:, :, i * D + oc * 128:i * D + (oc + 1) * 128],
              writes=[("bwg", oc)])
        K.dma("pool", wb[:, :, oc * 128:(oc + 1) * 128], wbv[:, :, oc * 128:(oc + 1) * 128], writes=[("bwb", oc)])
    K.dma("pool", wo[:], w_o_d.rearrange("(kc p) c -> p kc c", p=128), writes=[("bwo",)])
    for tg in range(4):
        hk = [("hT", tt) for tt in range(tg * 4, tg * 4 + 4)]
        yk = [("yT", c_, jl) for c_ in range(4) for jl in range(tg * 4, tg * 4 + 4)]
        for oc in range(8):
            b = K.rr("br_b", 2)
            for kc in range(8):
                K.mm(pb[:, b, :], wg[:, kc, oc * 128:(oc + 1) * 128], hT[:, kc, tg * 512:(tg + 1) * 512], kc == 0, kc == 7,
                     hk + [("bwg", oc)], [("pb", b)])
            for ec in range(4):
                K.mm(pb[:, 2 + b, :], wb[:, ec, oc * 128:(oc + 1) * 128], yT[:, ec, tg * 512:(tg + 1) * 512], ec == 0, ec == 3,
                     yk + [("bwb", oc)], [("pb", 2 + b)])
            K.act(sig[:, b, :], pb[:, b, :], AF.Sigmoid, [("pb", b), ("bgate",)], [("sig", b)],
                  bias=bgate[:, i * 8 + oc:i * 8 + oc + 1])
            K.tt("dve", mT[:, oc, :], sig[:, b, :], pb[:, 2 + b, :], ALU.mult, [("sig", b), ("pb", 2 + b)], [("mT", oc)])
        for t4 in range(4):
            tt = tg * 4 + t4
            for dh in range(2):
                yb = 4 + K.rr("br_y", 2)
                for oc in range(8):
                    K.mm(pb[:, yb, :], mT[:, oc, t4 * 128:(t4 + 1) * 128], wo[:, oc, dh * 512:(dh + 1) * 512], oc == 0, oc == 7,
                         [("mT", oc), ("bwo",)], [("pb", yb)])
                K.tt("dve", x_sb[:, tt, dh * 512:(dh + 1) * 512], pb[:, yb, :], x_sb[:, tt, dh * 512:(dh + 1) * 512], ALU.add,
                     [("pb", yb), ("x", tt, dh)], [("x", tt, dh)])
    K.release(m)


def emit_qproj(K, C, qT, w_in_d, kind):
    def cols(ci):
        if kind == "A":
            return w_in_d[:, QA + 128 * ci:QA + 128 * (ci + 1)]
        if kind == "D":
            return w_in_d[:, QD + 128 * ci:QD + 128 * (ci + 1)]
        return (w_in_d[:, QC + 64 * ci:QC + 64 * (ci + 1)], w_in_d[:, QC + 64 * (ci + 4):QC + 64 * (ci + 5)])

    def dst_fn(ci, tg):
        return qT[:, ci, tg * 512:(tg + 1) * 512], [("qT", ci, tg)]

    emit_proj_T(K, C, cols, 4, dst_fn, 0.125, "q")


def emit_mixer(K, C, I, NC):
    m = K.mark()
    qT = K.sb([128, 4, TOK], BF16)
    yT = K.sb([128, 4, TOK], BF16)
    small = K.sb([128, 256 + 2 + 128 + 8 + 24 + 8 + 4 * (2 + NC)], F32)
    lamraw = small[:, 0:256]
    consts = small[:, 256:258]
    gsub = small[:, 258:259]
    sinkexp = small[:, 386:394]
    bgate = small[:, 394:418]
    lamw = small[:, 418:426]
    fb = small[:, 426:426 + 4 * (2 + NC)].rearrange("p (h c) -> p h c", h=4)
    lprod = K.sb([128, 128], F32)
    K.dma("sp", lamraw, I["lam"].broadcast_to([128, 256]), writes=[("lamraw",)])
    K.dma("sp", consts, I["consts"].broadcast_to([128, 2]), writes=[("consts",)])
    K.dma("sp", gsub, I["subln"], writes=[("gsubraw",)])
    K.dma("sp", sinkexp, I["sink"].broadcast_to([128, 8]), writes=[("sinkraw",)])
    K.dma("sp", bgate, I["bgate"], writes=[("bgate",)])
    K.dma("sp", fb, I["fb"], writes=[("fb",)])
    for j in range(2):
        K.P.op("dve", lambda e, j=j: e.tensor_tensor(out=lprod[:, j * 64:(j + 1) * 64], in0=lamraw[:, j * 128:j * 128 + 64],
                                                    in1=lamraw[:, j * 128 + 64:j * 128 + 128], op=ALU.mult),
               [("lamraw",)], [("lprod", j)])
        K.P.op("dve", lambda e, j=j: e.reduce_sum(out=lamw[:, j:j + 1], in_=lprod[:, j * 64:(j + 1) * 64],
                                                 axis=mybir.AxisListType.X), [("lprod", j)], [("lamw", j)])
        K.act(lamw[:, 2 + j:3 + j], lamw[:, j:j + 1], AF.Exp, [("lamw", j)], [("lamw", 2 + j)])
    K.tt("dve", lamw[:, 4:5], lamw[:, 3:4], lamw[:, 2:3], ALU.subtract, [("lamw", 2), ("lamw", 3)], [("lamw", 4)])
    K.tt("dve", lamw[:, 5:6], lamw[:, 4:5], consts[:, 0:1], ALU.subtract, [("lamw", 4), ("consts",)], [("lam",)])
    lamneg = lamw[:, 5:6]
    K.ts("dve", gsub, gsub, consts[:, 1:2], None, ALU.mult, None, [("gsubraw",), ("consts",)], [("gsub",)])
    K.act(sinkexp, sinkexp, AF.Exp, [("sinkraw",)], [("sinkexp",)])

    FL = ()
    if "noA" not in FL:
        emit_qproj(K, C, qT, I["w_in"], "A")
        emit_attn_A(K, C, qT, yT, I["kT_all"], I["vA_all"], I["biasA"], I["maskA"], NC)
    if "nobr" not in FL:
        emit_branch(K, C, yT, 0, I["w_gate"], bgate, I["w_branch"], I["w_o"])
    if "noD" not in FL:
        emit_qproj(K, C, qT, I["w_in"], "D")
        emit_attn_diff(K, C, qT, yT, I["kT_all"], I["vD_all"], I["biasD"], fb, lamneg, gsub, NC)
    if "nobr" not in FL:
        emit_branch(K, C, yT, 1, I["w_gate"], bgate, I["w_branch"], I["w_o"])
    if "noC" not in FL:
        emit_qproj(K, C, qT, I["w_in"], "C")
        emit_attn_C(K, C, qT, yT, I["kT_all"], I["vC_all"], I["biasC"], I["maskC"], sinkexp, NC)
    if "nobr" not in FL:
        emit_branch(K, C, yT, 2, I["w_gate"], bgate, I["w_branch"], I["w_o"])
    K.release(m)


def emit_final_norm(K, C, g_row_d):
    x_sb, ssq, rstd, junk, gbc = (C[k] for k in ("x", "ssq", "rstd", "junk", "gbc"))
    K.dma("sp", gbc[:], g_row_d.broadcast_to([128, D]), writes=[("gbc",)])
    for tt in range(NT):
        xk = [("x", tt, 0), ("x", tt, 1)]
        K.act(junk[:], x_sb[:, tt, :], AF.Square, xk, [("junk",), ("ssq", tt)], accum_out=ssq[:, tt:tt + 1])
        K.ts("dve", rstd[:, tt:tt + 1], ssq[:, tt:tt + 1], 1.0 / D, EPS, ALU.mult, ALU.add, [("ssq", tt)], [("rstd", tt)])
        K.P.op("act", lambda e, tt=tt: e.sqrt(out=rstd[:, tt:tt + 1], in_=rstd[:, tt:tt + 1]), [("rstd", tt)], [("rstd", tt)])
        K.recip(rstd[:, tt:tt + 1], rstd[:, tt:tt + 1], [("rstd", tt)], [("rstd", tt)])
        K.stt("dve", x_sb[:, tt, :], x_sb[:, tt, :], rstd[:, tt:tt + 1], gbc[:], ALU.mult, ALU.mult,
              xk + [("rstd", tt), ("gbc",)], xk)


def finish(K, C, out_d):
    ev = K.dma("sp", out_d.rearrange("(tt p) d -> p tt d", p=128), C["x"][:], reads=XKEYS, slot=K.kvslot(("xout",)))
    K.pending.append(ev)
    for ev in K.pending:
        K.P._emit_wait("sp", ev)
    K.P.emit()


def _kvslot(K, key):
    k = ("kvslot", key)
    if k not in K.P.esem:
        K.P.esem[k] = Slot(K.P.new_sem("o_" + "_".join(str(t) for t in key)))
    return K.P.esem[k]


def new_kb():
    K = KB()
    K.pending = []
    K.kvslot = lambda key: _kvslot(K, key)
    return K


def build_A():
    K = new_kb()
    x_d = K.din("x", [TOK, D])
    ident_d = K.din("ident", [128, 128])
    g_d = K.din("norm_g", [3, D])
    wg_d, wu_d, wd_d = K.din("wg", [D, DFF]), K.din("wu", [D, DFF]), K.din("wd", [DFF, D])
    w_in_d = K.din("w_in", [D, WIN])
    x1_d = K.dout("x1", [TOK, D])
    kT_o = K.dout("kT", [9, 128, TOK], BF16)
    vA_o = K.dout("vA", [8, 128, NT, 65], BF16)
    vD_o = K.dout("vD", [4, 128, NT, 129], BF16)
    vC_o = K.dout("vC", [2, 128, NT, 65], BF16)
    C = setup_base(K, x_d, ident_d)
    emit_norm_T(K, C, g_d[0:1, :])
    emit_ffn(K, C, wg_d, wu_d, wd_d)
    emit_norm_T(K, C, g_d[1:2, :])
    emit_kv(K, C, w_in_d, kT_o, vA_o, vD_o, vC_o)
    for key in (("kst", 0), ("kst", 1), ("vA",), ("vD",), ("vC",)):
        sl = K.kvslot(key)
        K.pending.append(('dma', sl, sl.count))
    finish(K, C, x1_d)
    return K


def mixer_inputs(K, NC):
    I = {}
    I["w_in"] = K.din("w_in", [D, WIN])
    I["w_gate"] = K.din("w_gate", [D, 3 * D])
    I["w_branch"] = K.din("w_branch", [3, 512, D])
    I["w_o"] = K.din("w_o", [D, D])
    I["bgate"] = K.din("bgate", [128, 24])
    I["lam"] = K.din("lam", [1, 256])
    I["consts"] = K.din("consts", [1, 2])
    I["subln"] = K.din("subln", [128, 1])
    I["sink"] = K.din("sink", [1, 8])
    I["fb"] = K.din("fb", [128, 4, 2 + NC])
    I["biasA"] = K.din("biasA", [8, 128, 896])
    I["maskA"] = K.din("maskA", [5, 128, 896])
    I["biasD"] = K.din("biasD", [4, 8, 128, 512])
    I["biasC"] = K.din("biasC", [6, 128, 512])
    I["maskC"] = K.din("maskC", [9, 128, 512])
    I["kT_all"] = K.din("kT_all", [NC, 9, 128, TOK], BF16)
    I["vA_all"] = K.din("vA_all", [NC, 8, 128, NT, 65], BF16)
    I["vD_all"] = K.din("vD_all", [NC, 4, 128, NT, 129], BF16)
    I["vC_all"] = K.din("vC_all", [NC, 2, 128, NT, 65], BF16)
    return I


def build_B(NC, last):
    K = new_kb()
    x_d = K.din("x", [TOK, D])
    ident_d = K.din("ident", [128, 128])
    g_d = K.din("norm_g", [3, D])
    wg_d, wu_d, wd_d = K.din("wg", [D, DFF]), K.din("wu", [D, DFF]), K.din("wd", [DFF, D])
    I = mixer_inputs(K, NC)
    if last:
        gf_d = K.din("final_g", [1, D])
    else:
        gn_d = K.din("norm_g_n", [3, D])
        wgn_d, wun_d, wdn_d = K.din("wg_n", [D, DFF]), K.din("wu_n", [D, DFF]), K.din("wd_n", [DFF, D])
        w_in_n = K.din("w_in_n", [D, WIN])
        kT_o = K.dout("kT", [9, 128, TOK], BF16)
        vA_o = K.dout("vA", [8, 128, NT, 65], BF16)
        vD_o = K.dout("vD", [4, 128, NT, 129], BF16)
        vC_o = K.dout("vC", [2, 128, NT, 65], BF16)
    out_d = K.dout("xo", [TOK, D])
    C = setup_base(K, x_d, ident_d)
    emit_norm_T(K, C, g_d[1:2, :])
    emit_mixer(K, C, I, NC)
    emit_norm_T(K, C, g_d[2:3, :])
    emit_ffn(K, C, wg_d, wu_d, wd_d)
    if last:
        emit_final_norm(K, C, gf_d)
    else:
        emit_norm_T(K, C, gn_d[0:1, :])
        emit_ffn(K, C, wgn_d, wun_d, wdn_d)
        emit_norm_T(K, C, gn_d[1:2, :])
        emit_kv(K, C, w_in_n, kT_o, vA_o, vD_o, vC_o)
        for key in (("kst", 0), ("kst", 1), ("vA",), ("vD",), ("vC",)):
            sl = K.kvslot(key)
            K.pending.append(('dma', sl, sl.count))
    finish(K, C, out_d)
    return K


def _t5_bucket(rel):
    import jax
    import jax.numpy as jnp
    with jax.default_device(jax.devices("cpu")[0]):
        rel = jnp.asarray(rel, jnp.int32)
        half, max_exact = 16, 8
        ret = (rel > 0).astype(jnp.int32) * half
        n = jnp.abs(rel)
        nf = jnp.maximum(n, 1).astype(jnp.float32)
        large = max_exact + (jnp.log(nf / max_exact) / math.log(128 / max_exact) * (half - max_exact)).astype(jnp.int32)
        large = jnp.minimum(large, half - 1)
        return np.asarray(ret + jnp.where(n < max_exact, n, large))


def host_tables(NC, rel_bias_table, na_rpb_l=None):
    S = NC * TOK
    rows = S // 64
    NBLK = S // 128
    tab = np.asarray(rel_bias_table, np.float32)
    k = np.arange(128)[:, None]
    out = {}
    q512 = np.arange(512)[None, :]
    relD = np.stack([o * 128 + k - q512 for o in range(-1, 5)])
    bD = _t5_bucket(relD)
    biasD, fb = [], []
    for c in range(NC):
        pc, nx = (c - 1) % NC, (c + 1) % NC
        rel_prev = (pc - c) * TOK + 15 * 128 + k - q512
        rel_next = (nx - c) * TOK + k - (1536 + q512)
        be = _t5_bucket(np.stack([rel_prev, rel_next]))
        bidx = np.concatenate([bD, be], 0)
        biasD.append(np.stack([tab[bidx, h] for h in range(4)]))
        f = np.zeros((4, 2 + NC), np.float32)
        for h in range(4):
            f[h, 0] = tab[15, h]
            f[h, 1] = tab[31, h]
            for i in range(NC):
                kc_ = (c + i) % NC
                f[h, 2 + i] = tab[31, h] if kc_ > c else tab[15, h]
        fb.append(np.broadcast_to(f[None], (128, 4, 2 + NC)).copy())
    out["biasD"], out["fb"] = biasD, fb
    q128 = np.arange(128)[None, :]
    relC = np.stack([o * 128 + k - q128 for o in (-1, 0, 1)])
    bC = _t5_bucket(relC)
    biasC = np.zeros((6, 128, 4, 128), np.float32)
    for g in range(2):
        for oi in range(3):
            for hq in range(4):
                biasC[g * 3 + oi, :, hq, :] = tab[bC[oi], 4 + g * 4 + hq]
    out["biasC"] = biasC.reshape(6, 128, 512)
    okC = (np.abs(relC) <= 128)
    mi = np.where(okC, 0.0, NEG).astype(np.float32)
    maskC = []
    for c in range(NC):
        sets = np.stack([mi, mi, mi])
        if c == 0:
            sets[1, 0] = NEG
        if c == NC - 1:
            sets[2, 2] = NEG
        mc = np.broadcast_to(sets.reshape(9, 128, 1, 128), (9, 128, 4, 128)).reshape(9, 128, 512)
        maskC.append(np.ascontiguousarray(mc))
    out["maskC"] = maskC
    kr2, kcol = (np.arange(128) // 64)[:, None, None], (np.arange(128) % 64)[:, None, None]
    qr2, qcol = (np.arange(128) // 64)[None, None, :], (np.arange(128) % 64)[None, None, :]
    o7 = np.arange(-3, 4)[None, :, None]
    dr_idx = (2 * o7 + kr2 - qr2 + 7) + 0 * kcol + 0 * qcol
    dc_idx = np.clip(kcol - qcol + 15, 0, 30) + 0 * o7
    out["A_idx"] = (np.broadcast_to(dr_idx, (128, 7, 128)), np.broadcast_to(dc_idx, (128, 7, 128)))
    cs = np.clip(qcol - 8, 0, 64 - 16)
    col_ok = (kcol >= cs) & (kcol < cs + 16)

    def maskA_for(j):
        r = 2 * j + qr2
        kr = 2 * (j + o7) + kr2
        rs = np.clip(r - 4, 0, rows - 8)
        ok = (kr >= rs) & (kr < rs + 8) & col_ok
        return np.where(ok, 0.0, NEG).astype(np.float32).reshape(128, 896)

    jmid = NBLK // 2
    maskA = []
    for c in range(NC):
        sets = [maskA_for(jmid)]
        for jl in (0, 1, NT - 2, NT - 1):
            j = c * NT + jl
            sets.append(maskA_for(j) if (j < 2 or j >= NBLK - 2) else maskA_for(jmid))
        maskA.append(np.stack(sets))
    out["maskA"] = maskA
    return out


def gather_biasA(tables, rpb_l):
    dr_idx, dc_idx = tables["A_idx"]
    rpb_l = np.asarray(rpb_l, np.float32)
    return np.ascontiguousarray(rpb_l[:, dr_idx, dc_idx].reshape(8, 128, 896))


_PROG_CACHE = {}


def _prog(key, fn):
    if key not in _PROG_CACHE:
        _PROG_CACHE[key] = fn()
    return _PROG_CACHE[key]


def run_layers(x, w_in, w_branch, w_gate, b_gate, w_o, norm_g, final_g, ffn_w_gate, ffn_w_up, ffn_w_down,
               na_rpb, diff_lambda, diff_subln_g, gqa_sink, rel_bias_table, NC, depth):
    f32 = lambda a: np.ascontiguousarray(np.asarray(a, np.float32))
    x = f32(x).reshape(NC * TOK, D)
    ident = np.eye(128, dtype=np.float32)
    tables = host_tables(NC, rel_bias_table)
    cores = list(range(NC))
    xs = [x[c * TOK:(c + 1) * TOK] for c in cores]
    def ffn_w(l, j, suf=""):
        return {"wg" + suf: f32(ffn_w_gate[l, j]), "wu" + suf: f32(ffn_w_up[l, j]), "wd" + suf: f32(ffn_w_down[l, j])}

    KA_ = _prog("A", build_A)
    in_maps = [{"x": xs[c], "ident": ident, "norm_g": f32(norm_g[0]), "w_in": f32(w_in[0]), **ffn_w(0, 0)} for c in cores]
    res = run_bass_kernel_spmd(KA_.nc, in_maps, core_ids=cores).results
    xs = [res[c]["x1"] for c in cores]
    for l in range(depth):
        comp = {n: np.stack([res[c][n] for c in cores]) for n in ("kT", "vA", "vD", "vC")}
        last = (l == depth - 1)
        KB_ = _prog(("B", NC, last), lambda: build_B(NC, last))
        lam_init = 0.8 - 0.6 * math.exp(-0.3 * l)
        biasA = gather_biasA(tables, na_rpb[l])
        shared = {
            "ident": ident, "norm_g": f32(norm_g[l]), **ffn_w(l, 1),
            "w_in": f32(w_in[l]), "w_gate": f32(w_gate[l]), "w_branch": f32(w_branch[l]), "w_o": f32(w_o[l]),
            "bgate": np.ascontiguousarray(f32(b_gate[l]).reshape(24, 128).T),
            "lam": f32(diff_lambda[l]).reshape(1, 256),
            "consts": np.array([[lam_init, 1.0 - lam_init]], np.float32),
            "subln": f32(diff_subln_g[l]).reshape(128, 1), "sink": f32(gqa_sink[l]).reshape(1, 8),
            "biasA": biasA, "biasC": tables["biasC"],
        }
        if last:
            shared["final_g"] = f32(final_g).reshape(1, D)
        else:
            shared.update({"norm_g_n": f32(norm_g[l + 1]), "w_in_n": f32(w_in[l + 1]), **ffn_w(l + 1, 0, "_n")})
        in_maps = []
        for c in cores:
            rot = lambda a: np.ascontiguousarray(np.roll(a, -c, axis=0))
            in_maps.append({
                "x": xs[c], **shared,
                "fb": tables["fb"][c], "maskA": tables["maskA"][c], "biasD": tables["biasD"][c], "maskC": tables["maskC"][c],
                "kT_all": rot(comp["kT"]), "vA_all": rot(comp["vA"]), "vD_all": rot(comp["vD"]), "vC_all": rot(comp["vC"]),
            })
        res = run_bass_kernel_spmd(KB_.nc, in_maps, core_ids=cores).results
        xs = [res[c]["xo"] for c in cores]
    return np.concatenate(xs, 0).reshape(1, NC * TOK, D).astype(np.float32)


def kernel(x, w_in, w_branch, w_gate, b_gate, w_o, norm_g, final_g, ffn_w_gate, ffn_w_up, ffn_w_down,
           na_rpb, diff_lambda, diff_subln_g, gqa_sink, rel_bias_table):
    return run_layers(x, w_in, w_branch, w_gate, b_gate, w_o, norm_g, final_g, ffn_w_gate, ffn_w_up, ffn_w_down,
                      na_rpb, diff_lambda, diff_subln_g, gqa_sink, rel_bias_table, NCORES, DEPTH)
```

```python
import math
import numpy as np
import ml_dtypes
import concourse.bass as bass
import concourse.mybir as mybir
from concourse.bass_utils import run_bass_kernel_spmd

F32 = mybir.dt.float32
BF16 = mybir.dt.bfloat16
AF = mybir.ActivationFunctionType
ALU = mybir.AluOpType

D = 1024
DFF = 2816
NT = 16
TOK = NT * 128
EPS = 1e-6
DEPTH = 4
NCORES = 8
NEG = -30000.0
WIN = 3840
QA, KA, VA, QD, KD, VD, QC, KC, VC = 0, 512, 1024, 1536, 2048, 2560, 3072, 3584, 3712

ENGS = ("pe", "act", "dve", "pool", "sp")
EPOCH = 24000


class Slot:
    def __init__(self, sem):
        self.sem = sem
        self.count = 0


class Prog:
    def __init__(self, nc):
        self.nc = nc
        self.ops = {e: [] for e in ENGS}
        self.cnt = {e: 0 for e in ENGS}
        self.esem = {}
        self.waited = {e: {} for e in ENGS}
        self.waited_eng = {e: {} for e in ENGS}
        self.last_w = {}
        self.readers = {}
        self.nsem = 0

    def new_sem(self, name):
        self.nsem += 1
        return self.nc.alloc_semaphore(name)

    def _eng_sem(self, eng, epoch):
        k = (eng, epoch)
        if k not in self.esem:
            self.esem[k] = self.new_sem(f"e_{eng}_{epoch}")
        return self.esem[k]

    def _emit_wait(self, consumer, ev):
        if ev is None:
            return
        if ev[0] == 'eng':
            _, eng, n = ev
            if eng == consumer and eng == 'pe':
                return
            prev = self.waited_eng[consumer].get(eng, 0)
            if prev >= n:
                return
            self.waited_eng[consumer][eng] = n
            epoch = (n - 1) // EPOCH
            val = n - epoch * EPOCH
            sem = self._eng_sem(eng, epoch)
            self.ops[consumer].append(lambda e, sem=sem, val=val: e.wait_ge(sem, val))
        else:
            _, slot, count = ev
            key = id(slot)
            prev = self.waited[consumer].get(key, 0)
            if prev >= count:
                return
            self.waited[consumer][key] = count
            self.ops[consumer].append(lambda e, sem=slot.sem, val=count: e.wait_ge(sem, val))

    def _deps(self, consumer, reads, writes):
        for k in reads:
            self._emit_wait(consumer, self.last_w.get(k))
        for k in writes:
            self._emit_wait(consumer, self.last_w.get(k))
            for ev in self.readers.get(k, ()):
                if ev[0] == 'eng' and ev[1] == consumer:
                    continue
                self._emit_wait(consumer, ev)

    def _record(self, ev, reads, writes):
        for k in reads:
            lst = self.readers.setdefault(k, [])
            lst.append(ev)
            if len(lst) > 12:
                comp = {}
                for e2 in lst:
                    kk = (e2[0], e2[1] if e2[0] == 'eng' else id(e2[1]))
                    if kk not in comp or comp[kk][2] < e2[2]:
                        comp[kk] = e2
                self.readers[k] = list(comp.values())
        for k in writes:
            self.last_w[k] = ev
            self.readers[k] = []

    def op(self, eng, fn, reads=(), writes=()):
        self._deps(eng, reads, writes)
        self.cnt[eng] += 1
        n = self.cnt[eng]
        sem = self._eng_sem(eng, (n - 1) // EPOCH)
        self.ops[eng].append(lambda e, fn=fn, sem=sem: fn(e).then_inc(sem, 1))
        self._record(('eng', eng, n), reads, writes)

    def dma(self, queue, out, in_, reads=(), writes=(), slot=None):
        if slot is None:
            k = ("slot", writes[0] if writes else ("rd", reads[0]))
            if k not in self.esem:
                self.esem[k] = Slot(self.new_sem("d_" + "_".join(str(t) for t in k[1])))
            slot = self.esem[k]
        if slot.count:
            self._emit_wait(queue, ('dma', slot, slot.count))
        self._deps(queue, reads, writes)
        slot.count += 16
        self.ops[queue].append(
            lambda e, out=out, in_=in_, sem=slot.sem: e.dma_start(out=out, in_=in_).then_inc(sem, 16))
        ev = ('dma', slot, slot.count)
        self._record(ev, reads, writes)
        return ev

    def barrier(self):
        slots = [v for v in self.esem.values() if isinstance(v, Slot)]
        for c in ENGS:
            for p in ENGS:
                if self.cnt[p] and not (p == c and p in ("pe", "sp")):
                    self._emit_wait(c, ('eng', p, self.cnt[p]))
            for sl in slots:
                if sl.count:
                    self._emit_wait(c, ('dma', sl, sl.count))

    def emit(self):
        nc, ops = self.nc, self.ops
        with nc.Block() as block:
            @block.tensor
            def _(e):
                for f in ops["pe"]:
                    f(e)

            @block.scalar
            def _(e):
                for f in ops["act"]:
                    f(e)

            @block.vector
            def _(e):
                for f in ops["dve"]:
                    f(e)

            @block.gpsimd
            def _(e):
                for f in ops["pool"]:
                    f(e)

            @block.sync
            def _(e):
                for f in ops["sp"]:
                    f(e)


class KB:
    SB_LO = 16640
    SB_HI = 229376

    def __init__(self):
        self.nc = bass.Bass("TRN2", target_bir_lowering=False)
        self.P = Prog(self.nc)
        self.top = self.SB_LO
        self.nname = 0
        self.pb = self.nc.alloc_psum_tensor("pb", [128, 8, 512], F32)
        self.cnt = {}

    def mark(self):
        return self.top

    def release(self, m):
        self.top = m
        self.P.barrier()

    def sb(self, shape, dt):
        n = 1
        for s in shape[1:]:
            n *= s
        nbytes = n * (4 if dt == F32 else 2)
        nbytes = (nbytes + 63) // 64 * 64
        off = self.top
        self.top += nbytes
        assert self.top <= self.SB_HI, f"SBUF overflow {self.top}"
        self.nname += 1
        return self.nc.alloc_sbuf_tensor_at(f"t{self.nname}", list(shape), dt, offset=off)

    def din(self, name, shape, dt=F32):
        return self.nc.dram_tensor(name, list(shape), dt, kind="ExternalInput").ap()

    def dout(self, name, shape, dt=F32):
        return self.nc.dram_tensor(name, list(shape), dt, kind="ExternalOutput").ap()

    def rr(self, name, n):
        c = self.cnt.get(name, 0)
        self.cnt[name] = c + 1
        return c % n

    def mm(self, out, lhsT, rhs, start, stop, reads, writes, skip=False):
        if skip:
            self.P.op("pe", lambda e: e.matmul(out, lhsT=lhsT, rhs=rhs, start=start, stop=stop, skip_group_check=True),
                      reads, writes)
        else:
            self.P.op("pe", lambda e: e.matmul(out, lhsT=lhsT, rhs=rhs, start=start, stop=stop), reads, writes)

    def act(self, out, in_, func, reads, writes, bias=None, scale=None, accum_out=None):
        kw = {}
        if bias is not None:
            kw["bias"] = bias
        if scale is not None:
            kw["scale"] = scale
        if accum_out is not None:
            kw["accum_out"] = accum_out
        self.P.op("act", lambda e: e.activation(out=out, in_=in_, func=func, **kw), reads, writes)

    def ts(self, eng, out, in0, s1, s2, op0, op1, reads, writes):
        if op1 is None:
            self.P.op(eng, lambda e: e.tensor_scalar(out=out, in0=in0, scalar1=s1, scalar2=None, op0=op0), reads, writes)
        else:
            self.P.op(eng, lambda e: e.tensor_scalar(out=out, in0=in0, scalar1=s1, scalar2=s2, op0=op0, op1=op1),
                      reads, writes)

    def tt(self, eng, out, in0, in1, op, reads, writes):
        self.P.op(eng, lambda e: e.tensor_tensor(out=out, in0=in0, in1=in1, op=op), reads, writes)

    def stt(self, eng, out, in0, scalar, in1, op0, op1, reads, writes):
        self.P.op(eng, lambda e: e.scalar_tensor_tensor(out=out, in0=in0, scalar=scalar, in1=in1, op0=op0, op1=op1),
                  reads, writes)

    def copy(self, eng, out, in_, reads, writes):
        if eng == "act":
            self.P.op("act", lambda e: e.copy(out=out, in_=in_), reads, writes)
        else:
            self.P.op(eng, lambda e: e.tensor_copy(out=out, in_=in_), reads, writes)

    def recip(self, out, in_, reads, writes):
        self.P.op("dve", lambda e: e.reciprocal(out=out, in_=in_), reads, writes)

    def memset(self, eng, ap, val, writes):
        self.P.op(eng, lambda e: e.memset(ap, val), (), writes)

    def dma(self, q, out, in_, reads=(), writes=(), slot=None):
        return self.P.dma(q, out, in_, reads, writes, slot)


XKEYS = [("x", tt, dh) for tt in range(NT) for dh in range(2)]


def setup_base(K, x_d, ident_d):
    C = {}
    C["x"] = K.sb([128, NT, D], F32)
    C["hT"] = K.sb([128, 8, TOK], BF16)
    C["ssq"] = K.sb([128, NT], F32)
    C["rstd"] = K.sb([128, NT], F32)
    C["xn"] = K.sb([128, 2, D], BF16)
    C["junk"] = K.sb([128, D], BF16)
    C["ident"] = K.sb([128, 128], BF16)
    C["gbc"] = K.sb([128, D], F32)
    K.dma("sp", C["x"][:], x_d.rearrange("(tt p) d -> p tt d", p=128), writes=XKEYS)
    K.dma("pool", C["ident"][:], ident_d, writes=[("ident",)])
    return C


def emit_norm_T(K, C, g_row_d):
    x_sb, hT, ssq, rstd, xn, junk, ident, gbc = (C[k] for k in ("x", "hT", "ssq", "rstd", "xn", "junk", "ident", "gbc"))
    pb = K.pb
    K.dma("sp", gbc[:], g_row_d.broadcast_to([128, D]), writes=[("gbc",)])
    for tt in range(NT):
        xk = [("x", tt, 0), ("x", tt, 1)]
        K.act(junk[:], x_sb[:, tt, :], AF.Square, xk, [("junk",), ("ssq", tt)], accum_out=ssq[:, tt:tt + 1])
        K.ts("dve", rstd[:, tt:tt + 1], ssq[:, tt:tt + 1], 1.0 / D, EPS, ALU.mult, ALU.add, [("ssq", tt)], [("rstd", tt)])
        K.P.op("act", lambda e, tt=tt: e.sqrt(out=rstd[:, tt:tt + 1], in_=rstd[:, tt:tt + 1]), [("rstd", tt)], [("rstd", tt)])
        K.recip(rstd[:, tt:tt + 1], rstd[:, tt:tt + 1], [("rstd", tt)], [("rstd", tt)])
        s = tt % 2
        K.stt("dve", xn[:, s, :], x_sb[:, tt, :], rstd[:, tt:tt + 1], gbc[:], ALU.mult, ALU.mult,
              xk + [("rstd", tt), ("gbc",)], [("xn", s)])
        for kc in range(8):
            K.mm(pb[:, 6 + kc // 4, (kc % 4) * 128:(kc % 4 + 1) * 128], xn[:, s, kc * 128:(kc + 1) * 128], ident[:],
                 True, True, [("xn", s), ("ident",)], [("pb", 6 + kc // 4)])
        K.copy("act", hT[:, :, tt * 128:(tt + 1) * 128], pb[:, 6:8, :].rearrange("p b (k t) -> p (b k) t", t=128),
               [("pb", 6), ("pb", 7)], [("hT", tt)])


def emit_ffn(K, C, wg_d, wu_d, wd_d):
    x_sb, hT, pb = C["x"], C["hT"], K.pb
    m = K.mark()
    wg_sb = K.sb([128, 2, 8, 256], BF16)
    wu_sb = K.sb([128, 2, 8, 256], BF16)
    wd_sb = K.sb([128, 2, 2, D], BF16)
    sg = K.sb([128, 2, 512], F32)
    uT = K.sb([128, 2, 2, 512], BF16)
    FG = 256
    wg_v = wg_d.rearrange("(kc p) f -> p kc f", p=128)
    wu_v = wu_d.rearrange("(kc p) f -> p kc f", p=128)
    wd_v = wd_d.rearrange("(fc p) d -> p fc d", p=128)
    items = []
    wslot = {}

    def gateup(fg, tg):
        if tg == 0:
            ws = K.rr("ffn_w", 2)
            wslot[fg] = ws
            f0 = fg * FG
            K.dma("pool", wg_sb[:, ws, :, :], wg_v[:, :, f0:f0 + FG], writes=[("wg", ws)])
            K.dma("pool", wu_sb[:, ws, :, :], wu_v[:, :, f0:f0 + FG], writes=[("wu", ws)])
            K.dma("pool", wd_sb[:, ws, :, :], wd_v[:, 2 * fg:2 * fg + 2, :], writes=[("wd", ws)])
        ws = wslot[fg]
        us = K.rr("ffn_u", 2)
        hk = [("hT", tt) for tt in range(tg * 4, tg * 4 + 4)]
        for fc in range(2):
            b = K.rr("ffn_b", 2)
            for kc in range(8):
                K.mm(pb[:, b, :], wg_sb[:, ws, kc, fc * 128:(fc + 1) * 128], hT[:, kc, tg * 512:(tg + 1) * 512],
                     kc == 0, kc == 7, hk + [("wg", ws)], [("pb", b)])
            for kc in range(8):
                K.mm(pb[:, 2 + b, :], wu_sb[:, ws, kc, fc * 128:(fc + 1) * 128], hT[:, kc, tg * 512:(tg + 1) * 512],
                     kc == 0, kc == 7, hk + [("wu", ws)], [("pb", 2 + b)])
            K.act(sg[:, b, :], pb[:, b, :], AF.Silu, [("pb", b)], [("sg", b)])
            K.tt("dve", uT[:, us, fc, :], sg[:, b, :], pb[:, 2 + b, :], ALU.mult, [("sg", b), ("pb", 2 + b)],
                 [("uT", us, fc)])
        return us

    def down(fg, tg, us):
        ws = wslot[fg]
        for t4 in range(4):
            tt = tg * 4 + t4
            for dh in range(2):
                yb = 4 + K.rr("ffn_y", 2)
                for fc in range(2):
                    K.mm(pb[:, yb, :], uT[:, us, fc, t4 * 128:(t4 + 1) * 128], wd_sb[:, ws, fc, dh * 512:(dh + 1) * 512],
                         fc == 0, fc == 1, [("uT", us, fc), ("wd", ws)], [("pb", yb)])
                K.stt("dve", x_sb[:, tt, dh * 512:(dh + 1) * 512], pb[:, yb, :], 0.5,
                      x_sb[:, tt, dh * 512:(dh + 1) * 512], ALU.mult, ALU.add,
                      [("pb", yb), ("x", tt, dh)], [("x", tt, dh)])

    seq = [(fg, tg) for fg in range(DFF // FG) for tg in range(4)]
    prev = None
    for (fg, tg) in seq:
        us = gateup(fg, tg)
        if prev is not None:
            down(*prev)
        prev = (fg, tg, us)
    down(*prev)
    K.release(m)


def emit_proj_T(K, C, w_cols_d, ncols_chunks, dst_fn, scale, tagw):
    hT, pb = C["hT"], K.pb
    m = K.mark()
    wsb = K.sb([128, 2, 8, 128], BF16)
    for ci in range(ncols_chunks):
        ws = K.rr("pw" + tagw, 2)
        src = w_cols_d(ci)
        if isinstance(src, tuple):
            for hh, s_ in enumerate(src):
                K.dma("pool", wsb[:, ws, :, hh * 64:(hh + 1) * 64], s_.rearrange("(kc p) c -> p kc c", p=128),
                      writes=[("pw", tagw, ws)] if hh == 0 else [("pw2", tagw, ws)])
            wk = [("pw", tagw, ws), ("pw2", tagw, ws)]
        else:
            K.dma("pool", wsb[:, ws, :, :], src.rearrange("(kc p) c -> p kc c", p=128),
                  writes=[("pw", tagw, ws), ("pw2", tagw, ws)])
            wk = [("pw", tagw, ws)]
        for tg in range(4):
            b = K.rr("pj_b", 2)
            hk = [("hT", tt) for tt in range(tg * 4, tg * 4 + 4)]
            for kc in range(8):
                K.mm(pb[:, b, :], wsb[:, ws, kc, :], hT[:, kc, tg * 512:(tg + 1) * 512], kc == 0, kc == 7,
                     hk + wk, [("pb", b)])
            dst, keys = dst_fn(ci, tg)
            if scale is None:
                K.copy("act" if tg % 2 == 0 else "dve", dst, pb[:, b, :], [("pb", b)], keys)
            else:
                K.ts("dve", dst, pb[:, b, :], scale, None, ALU.mult, None, [("pb", b)], keys) if tg % 2 else \
                    K.P.op("act", lambda e, dst=dst, b=b: e.mul(out=dst, in_=pb[:, b, :], mul=scale), [("pb", b)], keys)
    K.release(m)


def emit_kv(K, C, w_in_d, kT_out, vA_out, vD_out, vC_out):
    hT, pb = C["hT"], K.pb
    m = K.mark()
    kst = K.sb([128, 2, TOK], BF16)
    kcols = [KA + 128 * i for i in range(4)] + [KD + 128 * i for i in range(4)] + [KC]

    def dst_fn(ci, tg):
        s = ci % 2
        return kst[:, s, tg * 512:(tg + 1) * 512], [("kst", s, tg)]

    hT_ = hT
    wsb = K.sb([128, 2, 8, 128], BF16)
    for ci in range(9):
        ws = K.rr("pwk", 2)
        K.dma("pool", wsb[:, ws, :, :], w_in_d[:, kcols[ci]:kcols[ci] + 128].rearrange("(kc p) c -> p kc c", p=128),
              writes=[("pwk", ws)])
        s = ci % 2
        for tg in range(4):
            b = K.rr("pj_b", 2)
            hk = [("hT", tt) for tt in range(tg * 4, tg * 4 + 4)]
            for kc in range(8):
                K.mm(pb[:, b, :], wsb[:, ws, kc, :], hT_[:, kc, tg * 512:(tg + 1) * 512], kc == 0, kc == 7,
                     hk + [("pwk", ws)], [("pb", b)])
            K.copy("act" if tg % 2 == 0 else "dve", kst[:, s, tg * 512:(tg + 1) * 512], pb[:, b, :], [("pb", b)],
                   [("kst", s, tg)])
        K.dma("sp", kT_out[ci], kst[:, s, :], reads=[("kst", s, tg) for tg in range(4)], slot=K.kvslot(("kst", s)))
    wv = K.sb([128, 8, 1152], BF16)
    w_v = w_in_d.rearrange("(kc p) c -> p kc c", p=128)
    K.dma("pool", wv[:, :, 0:512], w_v[:, :, VA:VA + 512], writes=[("wv", 0)])
    K.dma("pool", wv[:, :, 512:1024], w_v[:, :, VD:VD + 512], writes=[("wv", 1)])
    K.dma("pool", wv[:, :, 1024:1152], w_v[:, :, VC:VC + 128], writes=[("wv", 2)])
    vA_st = K.sb([128, 8, NT, 65], BF16)
    vD_st = K.sb([128, 4, NT, 129], BF16)
    vC_st = K.sb([128, 2, NT, 65], BF16)
    K.memset("pool", vA_st[:, :, :, 64:65], 1.0, [("vAones",)])
    K.memset("pool", vD_st[:, :, :, 128:129], 1.0, [("vDones",)])
    K.memset("pool", vC_st[:, :, :, 64:65], 1.0, [("vCones",)])
    for tt in range(NT):
        for (j, c0, nc_, st, nh, hd, key) in ((0, 0, 512, vA_st, 8, 64, "vA"), (1, 512, 512, vD_st, 4, 128, "vD"),
                                               (2, 1024, 128, vC_st, 2, 64, "vC")):
            b = 2 + K.rr("v_b", 3)
            for kc in range(8):
                K.mm(pb[:, b, 0:nc_], hT[:, kc, tt * 128:(tt + 1) * 128], wv[:, kc, c0:c0 + nc_], kc == 0, kc == 7,
                     [("hT", tt), ("wv", j)], [("pb", b)])
            K.copy("act" if (tt + j) % 2 else "dve", st[:, :, tt, 0:hd],
                   pb[:, b, 0:nc_].rearrange("p (h e) -> p h e", h=nh), [("pb", b)], [(key, tt)])
    for (st, outd, key, ones) in ((vA_st, vA_out, "vA", "vAones"), (vD_st, vD_out, "vD", "vDones"), (vC_st, vC_out, "vC", "vCones")):
        K.dma("sp", outd.rearrange("h p t e -> p h t e"), st[:], reads=[(key, tt) for tt in range(NT)] + [(ones,)],
              slot=K.kvslot((key,)))
    K.release(m)


def emit_yT(K, C, ytmp, ykey, yT, chunk, jl):
    pb, ident = K.pb, C["ident"]
    K.mm(pb[:, 7, 0:128], ytmp, ident[:], True, True, [ykey, ("ident",)], [("pb", 7)])
    K.copy("dve", yT[:, chunk, jl * 128:(jl + 1) * 128], pb[:, 7, 0:128], [("pb", 7)], [("yT", chunk, jl)])


def emit_attn_diff(K, C, qT, yT, kT_all, vD_all, biasD_d, fb, lamneg, gsubcol, NC):
    pb, ident = K.pb, C["ident"]
    m = K.mark()
    kD = K.sb([128, 2, TOK], BF16)
    vD = K.sb([128, 2, NT, 129], BF16)
    PT = K.sb([128, 3, 2, 512], BF16)
    bmat = K.sb([128, 8, 512], BF16)
    Ps = K.sb([128, 2, 512], F32)
    ones = K.sb([128, 128], F32)
    onesb = K.sb([128, 128], BF16)
    rden = K.sb([128, 2, 512], F32)
    at = K.sb([128, 2, 512], F32)
    rs = K.sb([128, 512], F32)
    K.memset("dve", ones[:], 1.0, [("ones",)])
    K.memset("dve", onesb[:], 1.0, [("onesb",)])
    for h in range(4):
        K.dma("pool", bmat[:], biasD_d[h].rearrange("m p q -> p m q"), writes=[("bmat",)])
        for T in range(4):
            qk = [("qT", h, T)]
            blocks = [(i, b) for i in range(NC) for b in range(NT)]
            slot_of = {}

            def load_shard(i):
                s_ = K.rr("dkv", 2)
                slot_of[i] = s_
                K.dma("sp", kD[:, s_, :], kT_all[i, 4 + h], writes=[("kD", s_)])
                K.dma("sp", vD[:, s_, :, :], vD_all[i, h], writes=[("vD", s_)])

            def classify(i, b):
                mat = None
                if i == 0:
                    o = b - 4 * T
                    if -1 <= o <= 4:
                        mat, bias = o + 1, 0.0
                    elif o < -1:
                        bias = fb[:, h, 0:1]
                    else:
                        bias = fb[:, h, 1:2]
                elif i == NC - 1 and b == NT - 1 and T == 0:
                    mat, bias = 6, 0.0
                elif i == 1 and b == 0 and T == 3:
                    mat, bias = 7, 0.0
                else:
                    bias = fb[:, h, 2 + i:3 + i]
                return mat, bias

            def scores(n):
                i, b = blocks[n]
                if b == 0:
                    load_shard(i)
                s_ = slot_of[i]
                mat, bias = classify(i, b)
                u = n % 2
                p3 = n % 3
                for c in range(2):
                    bank = 2 * u + c
                    K.mm(pb[:, bank, :], kD[c * 64:(c + 1) * 64, s_, b * 128:(b + 1) * 128],
                         qT[c * 64:(c + 1) * 64, h, T * 512:(T + 1) * 512], True, mat is None,
                         [("kD", s_)] + qk, [("pb", bank)])
                    if mat is not None:
                        K.mm(pb[:, bank, :], ident[:], bmat[:, mat, :], False, True, [("ident",), ("bmat",)],
                             [("pb", bank)])
                rd = [("pb", 2 * u), ("pb", 2 * u + 1)] + ([] if isinstance(bias, float) else [("fb",)])
                K.act(PT[:, p3, :, :], pb[:, 2 * u:2 * u + 2, :], AF.Exp, rd, [("PT", p3, 0), ("PT", p3, 1)], bias=bias)

            def values(n):
                i, b = blocks[n]
                s_ = slot_of[i]
                u = n % 3
                first, last = (n == 0), (n == len(blocks) - 1)
                for c in range(2):
                    K.mm(pb[:, 4 + c, :], vD[:, s_, b, 0:128], PT[:, u, c, :], first, last,
                         [("PT", u, c), ("vD", s_)], [("pb", 4 + c)])
                if first:
                    K.copy("dve", Ps[:, 0, :], PT[:, u, 0, :], [("PT", u, 0)], [("Ps", 0)])
                else:
                    K.tt("dve", Ps[:, 0, :], Ps[:, 0, :], PT[:, u, 0, :], ALU.add, [("Ps", 0), ("PT", u, 0)], [("Ps", 0)])
                K.mm(pb[:, 6, :], onesb[:], PT[:, u, 1, :], first, last, [("PT", u, 1), ("onesb",)], [("pb", 6)])

            scores(0)
            scores(1)
            for n in range(len(blocks)):
                if n + 2 < len(blocks):
                    scores(n + 2)
                values(n)
            K.mm(pb[:, 7, :], ones[:], Ps[:, 0, :], True, True, [("ones",), ("Ps", 0)], [("pb", 7)])
            K.recip(rden[:, 0, :], pb[:, 7, :], [("pb", 7)], [("rden", 0)])
            K.recip(rden[:, 1, :], pb[:, 6, :], [("pb", 6)], [("rden", 1)])
            K.tt("dve", at[:, 0, :], pb[:, 4, :], rden[:, 0, :], ALU.mult, [("pb", 4), ("rden", 0)], [("at", 0)])
            K.tt("dve", rden[:, 1, :], pb[:, 5, :], rden[:, 1, :], ALU.mult, [("pb", 5), ("rden", 1)], [("rden", 1)])
            K.stt("dve", at[:, 0, :], rden[:, 1, :], lamneg[:, 0:1], at[:, 0, :], ALU.mult, ALU.add,
                  [("rden", 1), ("at", 0), ("lam",)], [("at", 0)])
            K.act(at[:, 1, :], at[:, 0, :], AF.Square, [("at", 0)], [("at", 1)])
            K.mm(pb[:, 7, :], ones[:], at[:, 1, :], True, True, [("ones",), ("at", 1)], [("pb", 7)])
            K.ts("dve", rs[:], pb[:, 7, :], 1.0 / 128, EPS, ALU.mult, ALU.add, [("pb", 7)], [("rs",)])
            K.P.op("act", lambda e: e.sqrt(out=rs[:], in_=rs[:]), [("rs",)], [("rs",)])
            K.recip(rs[:], rs[:], [("rs",)], [("rs",)])
            K.stt("dve", yT[:, h, T * 512:(T + 1) * 512], at[:, 0, :], gsubcol[:, 0:1], rs[:], ALU.mult, ALU.mult,
                  [("at", 0), ("rs",), ("gsub",)], [("yT", h, jl) for jl in range(T * 4, T * 4 + 4)])
    K.release(m)


def emit_attn_A(K, C, qT, yT, kT_all, vA_all, biasA_d, maskA_d, NC):
    pb, ident = K.pb, C["ident"]
    m = K.mark()
    kA = K.sb([128, 2, 22 * 128], BF16)
    vA = K.sb([128, 2, 2, 22, 65], BF16)
    bA = K.sb([128, 2, 896], BF16)
    mA = K.sb([128, 5, 896], BF16)
    PT = K.sb([128, 2, 896], BF16)
    rc = K.sb([128, 2], F32)
    ytmp = K.sb([128, 2, 128], BF16)
    K.dma("pool", mA[:], maskA_d.rearrange("s p q -> p s q"), writes=[("mA",)])
    prev, nxt = NC - 1, (1 if NC > 1 else 0)
    for ch in range(4):
        s = K.rr("akv", 2)
        K.dma("sp", kA[:, s, 384:384 + TOK], kT_all[0, ch], writes=[("kA", s, 0)])
        K.dma("sp", kA[:, s, 0:384], kT_all[prev, ch][:, 13 * 128:16 * 128], writes=[("kA", s, 1)])
        K.dma("sp", kA[:, s, 384 + TOK:768 + TOK], kT_all[nxt, ch][:, 0:384], writes=[("kA", s, 2)])
        for hh in range(2):
            K.dma("sp", vA[:, s, hh, 3:19, :], vA_all[0, 2 * ch + hh], writes=[("vA", s, hh, 0)])
            K.dma("sp", vA[:, s, hh, 0:3, :], vA_all[prev, 2 * ch + hh][:, 13:16, :], writes=[("vA", s, hh, 1)])
            K.dma("sp", vA[:, s, hh, 19:22, :], vA_all[nxt, 2 * ch + hh][:, 0:3, :], writes=[("vA", s, hh, 2)])
        kk = [("kA", s, t) for t in range(3)]
        for hh in range(2):
            h = 2 * ch + hh
            K.dma("pool", bA[:, hh, :], biasA_d[h], writes=[("bA", hh)])
        itemsA = [(jl, hh) for jl in range(NT) for hh in range(2)]

        def scoresA(n):
            jl, hh = itemsA[n]
            mset = {0: 1, 1: 2, NT - 2: 3, NT - 1: 4}.get(jl, 0)
            pr = hh * 64
            u = n % 2
            b0 = 2 * u
            for oi in range(7):
                bank = b0 + oi // 4
                K.mm(pb[:, bank, (oi % 4) * 128:(oi % 4 + 1) * 128],
                     kA[pr:pr + 64, s, (jl + oi) * 128:(jl + oi + 1) * 128],
                     qT[pr:pr + 64, ch, jl * 128:(jl + 1) * 128], oi % 4 == 0, False, kk + [("qT", ch, jl // 4)],
                     [("pb", bank)], skip=True)
            for half, (c0, c1) in enumerate(((0, 512), (512, 896))):
                bank = b0 + half
                K.mm(pb[:, bank, 0:c1 - c0], ident[:], bA[:, hh, c0:c1], False, False, [("ident",), ("bA", hh)],
                     [("pb", bank)], skip=True)
                K.mm(pb[:, bank, 0:c1 - c0], ident[:], mA[:, mset, c0:c1], False, True, [("ident",), ("mA",)],
                     [("pb", bank)], skip=True)
                K.act(PT[:, u, c0:c1], pb[:, bank, 0:c1 - c0], AF.Exp, [("pb", bank)], [("PTA", u, half)])

        def valuesA(n):
            jl, hh = itemsA[n]
            u = n % 2
            yt = jl % 2
            a = K.rr("aacc", 2)
            accv = pb[:, 4 + a, 0:65]
            for oi in range(7):
                K.mm(accv, PT[:, u, oi * 128:(oi + 1) * 128], vA[:, s, hh, jl + oi, :], oi == 0, oi == 6,
                     [("PTA", u, oi // 4)] + [("vA", s, hh, t) for t in range(3)], [("pb", 4 + a)])
            K.recip(rc[:, a:a + 1], accv[:, 64:65], [("pb", 4 + a)], [("rcA", a)])
            K.ts("dve", ytmp[:, yt, hh * 64:(hh + 1) * 64], accv[:, 0:64], rc[:, a:a + 1], None, ALU.mult, None,
                 [("pb", 4 + a), ("rcA", a)], [("aytmp", yt)])
            if hh == 1:
                emit_yT(K, C, ytmp[:, yt, :], ("aytmp", yt), yT, ch, jl)

        scoresA(0)
        for n in range(len(itemsA)):
            if n + 1 < len(itemsA):
                scoresA(n + 1)
            valuesA(n)
    K.release(m)


def emit_attn_C(K, C, qT, yT, kT_all, vC_all, biasC_d, maskC_d, sinkexp, NC):
    pb, ident = K.pb, C["ident"]
    m = K.mark()
    kC = K.sb([128, 18 * 128], BF16)
    vC = K.sb([128, 2, 18, 65], BF16)
    bC = K.sb([128, 6, 512], BF16)
    mC = K.sb([128, 9, 512], BF16)
    PT = K.sb([128, 3, 512], BF16)
    den = K.sb([128, 8], F32)
    ytmp = K.sb([128, 2, 256], BF16)
    prev, nxt = NC - 1, (1 if NC > 1 else 0)
    K.dma("pool", bC[:], biasC_d.rearrange("m p q -> p m q"), writes=[("bC",)])
    K.dma("pool", mC[:], maskC_d.rearrange("m p q -> p m q"), writes=[("mC",)])
    K.dma("sp", kC[:, 128:128 + TOK], kT_all[0, 8], writes=[("kC", 0)])
    K.dma("sp", kC[:, 0:128], kT_all[prev, 8][:, 15 * 128:16 * 128], writes=[("kC", 1)])
    K.dma("sp", kC[:, 128 + TOK:256 + TOK], kT_all[nxt, 8][:, 0:128], writes=[("kC", 2)])
    for g in range(2):
        K.dma("sp", vC[:, g, 1:17, :], vC_all[0, g], writes=[("vC", g, 0)])
        K.dma("sp", vC[:, g, 0:1, :], vC_all[prev, g][:, 15:16, :], writes=[("vC", g, 1)])
        K.dma("sp", vC[:, g, 17:18, :], vC_all[nxt, g][:, 0:1, :], writes=[("vC", g, 2)])
    kk = [("kC", t) for t in range(3)]
    itemsC = [(jl, g, oi) for jl in range(NT) for g in range(2) for oi in range(3)]

    def scoresC(n):
        jl, g, oi = itemsC[n]
        mset = 1 if jl == 0 else (2 if jl == NT - 1 else 0)
        pr = g * 64
        u = n % 3
        K.mm(pb[:, u, :], kC[pr:pr + 64, (jl + oi) * 128:(jl + oi + 1) * 128],
             qT[pr:pr + 64, :, jl * 128:(jl + 1) * 128], True, False, kk + [("qT", c_, jl // 4) for c_ in range(4)],
             [("pb", u)])
        K.mm(pb[:, u, :], ident[:], bC[:, g * 3 + oi, :], False, False, [("ident",), ("bC",)], [("pb", u)])
        K.mm(pb[:, u, :], ident[:], mC[:, mset * 3 + oi, :], False, True, [("ident",), ("mC",)], [("pb", u)])
        K.act(PT[:, u, :], pb[:, u, :], AF.Exp, [("pb", u)], [("PTC", u)])

    def valuesC(n):
        jl, g, oi = itemsC[n]
        u = n % 3
        a = (jl * 2 + g) % 2
        yt = a
        for hq in range(4):
            K.mm(pb[:, 4 + a, hq * 128:hq * 128 + 65], PT[:, u, hq * 128:(hq + 1) * 128], vC[:, g, jl + oi, :],
                 oi == 0 and hq == 0, oi == 2, [("PTC", u)] + [("vC", g, t) for t in range(3)], [("pb", 4 + a)],
                 skip=True)
        if oi == 2:
            for hq in range(4):
                hg = g * 4 + hq
                accv = pb[:, 4 + a, hq * 128:hq * 128 + 65]
                K.tt("dve", den[:, hg:hg + 1], accv[:, 64:65], sinkexp[:, hg:hg + 1], ALU.add,
                     [("pb", 4 + a), ("sinkexp",)], [("den", hg)])
                K.recip(den[:, hg:hg + 1], den[:, hg:hg + 1], [("den", hg)], [("den", hg)])
                K.ts("dve", ytmp[:, yt, hq * 64:(hq + 1) * 64], accv[:, 0:64], den[:, hg:hg + 1], None, ALU.mult, None,
                     [("pb", 4 + a), ("den", hg)], [("cytmp", yt)])
            for half in range(2):
                emit_yT(K, C, ytmp[:, yt, half * 128:(half + 1) * 128], ("cytmp", yt), yT, g * 2 + half, jl)

    scoresC(0)
    for n in range(len(itemsC)):
        if n + 1 < len(itemsC):
            scoresC(n + 1)
        valuesC(n)
    K.release(m)


def emit_branch(K, C, yT, i, w_gate_d, bgate, w_branch_d, w_o_d):
    x_sb, hT, pb = C["x"], C["hT"], K.pb
    m = K.mark()
    wg = K.sb([128, 8, D], BF16)
    wb = K.sb([128, 4, D], BF16)
    wo = K.sb([128, 8, D], BF16)
    sig = K.sb([128, 2, 512], F32)
    mT = K.sb([128, 8, 512], BF16)
    wgv = w_gate_d.rearrange("(kc p) c -> p kc c", p=128)
    wbv = w_branch_d[i].rearrange("(ec p) c -> p ec c", p=128)
    for oc in range(8):
        K.dma("pool", wg[:, :, oc * 128:(oc + 1) * 128], wgv[:, :, i * D + oc * 128:i * D + (oc + 1) * 128],
              writes=[("bwg", oc)])
        K.dma("pool", wb[:, :, oc * 128:(oc + 1) * 128], wbv[:, :, oc * 128:(oc + 1) * 128], writes=[("bwb", oc)])
    K.dma("pool", wo[:], w_o_d.rearrange("(kc p) c -> p kc c", p=128), writes=[("bwo",)])
    for tg in range(4):
        hk = [("hT", tt) for tt in range(tg * 4, tg * 4 + 4)]
        yk = [("yT", c_, jl) for c_ in range(4) for jl in range(tg * 4, tg * 4 + 4)]
        for oc in range(8):
            b = K.rr("br_b", 2)
            for kc in range(8):
                K.mm(pb[:, b, :], wg[:, kc, oc * 128:(oc + 1) * 128], hT[:, kc, tg * 512:(tg + 1) * 512], kc == 0, kc == 7,
                     hk + [("bwg", oc)], [("pb", b)])
            for ec in range(4):
                K.mm(pb[:, 2 + b, :], wb[:, ec, oc * 128:(oc + 1) * 128], yT[:, ec, tg * 512:(tg + 1) * 512], ec == 0, ec == 3,
                     yk + [("bwb", oc)], [("pb", 2 + b)])
            K.act(sig[:, b, :], pb[:, b, :], AF.Sigmoid, [("pb", b), ("bgate",)], [("sig", b)],
                  bias=bgate[:, i * 8 + oc:i * 8 + oc + 1])
            K.tt("dve", mT[:, oc, :], sig[:, b, :], pb[:, 2 + b, :], ALU.mult, [("sig", b), ("pb", 2 + b)], [("mT", oc)])
        for t4 in range(4):
            tt = tg * 4 + t4
            for dh in range(2):
                yb = 4 + K.rr("br_y", 2)
                for oc in range(8):
                    K.mm(pb[:, yb, :], mT[:, oc, t4 * 128:(t4 + 1) * 128], wo[:, oc, dh * 512:(dh + 1) * 512], oc == 0, oc == 7,
                         [("mT", oc), ("bwo",)], [("pb", yb)])
                K.tt("dve", x_sb[:, tt, dh * 512:(dh + 1) * 512], pb[:, yb, :], x_sb[:, tt, dh * 512:(dh + 1) * 512], ALU.add,
                     [("pb", yb), ("x", tt, dh)], [("x", tt, dh)])
    K.release(m)


def emit_qproj(K, C, qT, w_in_d, kind):
    def cols(ci):
        if kind == "A":
            return w_in_d[:, QA + 128 * ci:QA + 128 * (ci + 1)]
        if kind == "D":
            return w_in_d[:, QD + 128 * ci:QD + 128 * (ci + 1)]
        return (w_in_d[:, QC + 64 * ci:QC + 64 * (ci + 1)], w_in_d[:, QC + 64 * (ci + 4):QC + 64 * (ci + 5)])

    def dst_fn(ci, tg):
        return qT[:, ci, tg * 512:(tg + 1) * 512], [("qT", ci, tg)]

    emit_proj_T(K, C, cols, 4, dst_fn, 0.125, "q")


def emit_mixer(K, C, I, NC):
    m = K.mark()
    qT = K.sb([128, 4, TOK], BF16)
    yT = K.sb([128, 4, TOK], BF16)
    small = K.sb([128, 256 + 2 + 128 + 8 + 24 + 8 + 4 * (2 + NC)], F32)
    lamraw = small[:, 0:256]
    consts = small[:, 256:258]
    gsub = small[:, 258:259]
    sinkexp = small[:, 386:394]
    bgate = small[:, 394:418]
    lamw = small[:, 418:426]
    fb = small[:, 426:426 + 4 * (2 + NC)].rearrange("p (h c) -> p h c", h=4)
    lprod = K.sb([128, 128], F32)
    K.dma("sp", lamraw, I["lam"].broadcast_to([128, 256]), writes=[("lamraw",)])
    K.dma("sp", consts, I["consts"].broadcast_to([128, 2]), writes=[("consts",)])
    K.dma("sp", gsub, I["subln"], writes=[("gsubraw",)])
    K.dma("sp", sinkexp, I["sink"].broadcast_to([128, 8]), writes=[("sinkraw",)])
    K.dma("sp", bgate, I["bgate"], writes=[("bgate",)])
    K.dma("sp", fb, I["fb"], writes=[("fb",)])
    for j in range(2):
        K.P.op("dve", lambda e, j=j: e.tensor_tensor(out=lprod[:, j * 64:(j + 1) * 64], in0=lamraw[:, j * 128:j * 128 + 64],
                                                    in1=lamraw[:, j * 128 + 64:j * 128 + 128], op=ALU.mult),
               [("lamraw",)], [("lprod", j)])
        K.P.op("dve", lambda e, j=j: e.reduce_sum(out=lamw[:, j:j + 1], in_=lprod[:, j * 64:(j + 1) * 64],
                                                 axis=mybir.AxisListType.X), [("lprod", j)], [("lamw", j)])
        K.act(lamw[:, 2 + j:3 + j], lamw[:, j:j + 1], AF.Exp, [("lamw", j)], [("lamw", 2 + j)])
    K.tt("dve", lamw[:, 4:5], lamw[:, 3:4], lamw[:, 2:3], ALU.subtract, [("lamw", 2), ("lamw", 3)], [("lamw", 4)])
    K.tt("dve", lamw[:, 5:6], lamw[:, 4:5], consts[:, 0:1], ALU.subtract, [("lamw", 4), ("consts",)], [("lam",)])
    lamneg = lamw[:, 5:6]
    K.ts("dve", gsub, gsub, consts[:, 1:2], None, ALU.mult, None, [("gsubraw",), ("consts",)], [("gsub",)])
    K.act(sinkexp, sinkexp, AF.Exp, [("sinkraw",)], [("sinkexp",)])

    FL = ()
    if "noA" not in FL:
        emit_qproj(K, C, qT, I["w_in"], "A")
        emit_attn_A(K, C, qT, yT, I["kT_all"], I["vA_all"], I["biasA"], I["maskA"], NC)
    if "nobr" not in FL:
        emit_branch(K, C, yT, 0, I["w_gate"], bgate, I["w_branch"], I["w_o"])
    if "noD" not in FL:
        emit_qproj(K, C, qT, I["w_in"], "D")
        emit_attn_diff(K, C, qT, yT, I["kT_all"], I["vD_all"], I["biasD"], fb, lamneg, gsub, NC)
    if "nobr" not in FL:
        emit_branch(K, C, yT, 1, I["w_gate"], bgate, I["w_branch"], I["w_o"])
    if "noC" not in FL:
        emit_qproj(K, C, qT, I["w_in"], "C")
        emit_attn_C(K, C, qT, yT, I["kT_all"], I["vC_all"], I["biasC"], I["maskC"], sinkexp, NC)
    if "nobr" not in FL:
        emit_branch(K, C, yT, 2, I["w_gate"], bgate, I["w_branch"], I["w_o"])
    K.release(m)


def emit_final_norm(K, C, g_row_d):
    x_sb, ssq, rstd, junk, gbc = (C[k] for k in ("x", "ssq", "rstd", "junk", "gbc"))
    K.dma("sp", gbc[:], g_row_d.broadcast_to([128, D]), writes=[("gbc",)])
    for tt in range(NT):
        xk = [("x", tt, 0), ("x", tt, 1)]
        K.act(junk[:], x_sb[:, tt, :], AF.Square, xk, [("junk",), ("ssq", tt)], accum_out=ssq[:, tt:tt + 1])
        K.ts("dve", rstd[:, tt:tt + 1], ssq[:, tt:tt + 1], 1.0 / D, EPS, ALU.mult, ALU.add, [("ssq", tt)], [("rstd", tt)])
        K.P.op("act", lambda e, tt=tt: e.sqrt(out=rstd[:, tt:tt + 1], in_=rstd[:, tt:tt + 1]), [("rstd", tt)], [("rstd", tt)])
        K.recip(rstd[:, tt:tt + 1], rstd[:, tt:tt + 1], [("rstd", tt)], [("rstd", tt)])
        K.stt("dve", x_sb[:, tt, :], x_sb[:, tt, :], rstd[:, tt:tt + 1], gbc[:], ALU.mult, ALU.mult,
              xk + [("rstd", tt), ("gbc",)], xk)


def finish(K, C, out_d):
    ev = K.dma("sp", out_d.rearrange("(tt p) d -> p tt d", p=128), C["x"][:], reads=XKEYS, slot=K.kvslot(("xout",)))
    K.pending.append(ev)
    for ev in K.pending:
        K.P._emit_wait("sp", ev)
    K.P.emit()


def _kvslot(K, key):
    k = ("kvslot", key)
    if k not in K.P.esem:
        K.P.esem[k] = Slot(K.P.new_sem("o_" + "_".join(str(t) for t in key)))
    return K.P.esem[k]


def new_kb():
    K = KB()
    K.pending = []
    K.kvslot = lambda key: _kvslot(K, key)
    return K


def build_A():
    K = new_kb()
    x_d = K.din("x", [TOK, D])
    ident_d = K.din("ident", [128, 128])
    g_d = K.din("norm_g", [3, D])
    wg_d, wu_d, wd_d = K.din("wg", [D, DFF]), K.din("wu", [D, DFF]), K.din("wd", [DFF, D])
    w_in_d = K.din("w_in", [D, WIN])
    x1_d = K.dout("x1", [TOK, D])
    kT_o = K.dout("kT", [9, 128, TOK], BF16)
    vA_o = K.dout("vA", [8, 128, NT, 65], BF16)
    vD_o = K.dout("vD", [4, 128, NT, 129], BF16)
    vC_o = K.dout("vC", [2, 128, NT, 65], BF16)
    C = setup_base(K, x_d, ident_d)
    emit_norm_T(K, C, g_d[0:1, :])
    emit_ffn(K, C, wg_d, wu_d, wd_d)
    emit_norm_T(K, C, g_d[1:2, :])
    emit_kv(K, C, w_in_d, kT_o, vA_o, vD_o, vC_o)
    for key in (("kst", 0), ("kst", 1), ("vA",), ("vD",), ("vC",)):
        sl = K.kvslot(key)
        K.pending.append(('dma', sl, sl.count))
    finish(K, C, x1_d)
    return K


def mixer_inputs(K, NC):
    I = {}
    I["w_in"] = K.din("w_in", [D, WIN])
    I["w_gate"] = K.din("w_gate", [D, 3 * D])
    I["w_branch"] = K.din("w_branch", [3, 512, D])
    I["w_o"] = K.din("w_o", [D, D])
    I["bgate"] = K.din("bgate", [128, 24])
    I["lam"] = K.din("lam", [1, 256])
    I["consts"] = K.din("consts", [1, 2])
    I["subln"] = K.din("subln", [128, 1])
    I["sink"] = K.din("sink", [1, 8])
    I["fb"] = K.din("fb", [128, 4, 2 + NC])
    I["biasA"] = K.din("biasA", [8, 128, 896])
    I["maskA"] = K.din("maskA", [5, 128, 896])
    I["biasD"] = K.din("biasD", [4, 8, 128, 512])
    I["biasC"] = K.din("biasC", [6, 128, 512])
    I["maskC"] = K.din("maskC", [9, 128, 512])
    I["kT_all"] = K.din("kT_all", [NC, 9, 128, TOK], BF16)
    I["vA_all"] = K.din("vA_all", [NC, 8, 128, NT, 65], BF16)
    I["vD_all"] = K.din("vD_all", [NC, 4, 128, NT, 129], BF16)
    I["vC_all"] = K.din("vC_all", [NC, 2, 128, NT, 65], BF16)
    return I


def build_B(NC, last):
    K = new_kb()
    x_d = K.din("x", [TOK, D])
    ident_d = K.din("ident", [128, 128])
    g_d = K.din("norm_g", [3, D])
    wg_d, wu_d, wd_d = K.din("wg", [D, DFF]), K.din("wu", [D, DFF]), K.din("wd", [DFF, D])
    I = mixer_inputs(K, NC)
    if last:
        gf_d = K.din("final_g", [1, D])
    else:
        gn_d = K.din("norm_g_n", [3, D])
        wgn_d, wun_d, wdn_d = K.din("wg_n", [D, DFF]), K.din("wu_n", [D, DFF]), K.din("wd_n", [DFF, D])
        w_in_n = K.din("w_in_n", [D, WIN])
        kT_o = K.dout("kT", [9, 128, TOK], BF16)
        vA_o = K.dout("vA", [8, 128, NT, 65], BF16)
        vD_o = K.dout("vD", [4, 128, NT, 129], BF16)
        vC_o = K.dout("vC", [2, 128, NT, 65], BF16)
    out_d = K.dout("xo", [TOK, D])
    C = setup_base(K, x_d, ident_d)
    emit_norm_T(K, C, g_d[1:2, :])
    emit_mixer(K, C, I, NC)
    emit_norm_T(K, C, g_d[2:3, :])
    emit_ffn(K, C, wg_d, wu_d, wd_d)
    if last:
        emit_final_norm(K, C, gf_d)
    else:
        emit_norm_T(K, C, gn_d[0:1, :])
        emit_ffn(K, C, wgn_d, wun_d, wdn_d)
        emit_norm_T(K, C, gn_d[1:2, :])
        emit_kv(K, C, w_in_n, kT_o, vA_o, vD_o, vC_o)
        for key in (("kst", 0), ("kst", 1), ("vA",), ("vD",), ("vC",)):
            sl = K.kvslot(key)
            K.pending.append(('dma', sl, sl.count))
    finish(K, C, out_d)
    return K


def _t5_bucket(rel):
    import jax
    import jax.numpy as jnp
    with jax.default_device(jax.devices("cpu")[0]):
        rel = jnp.asarray(rel, jnp.int32)
        half, max_exact = 16, 8
        ret = (rel > 0).astype(jnp.int32) * half
        n = jnp.abs(rel)
        nf = jnp.maximum(n, 1).astype(jnp.float32)
        large = max_exact + (jnp.log(nf / max_exact) / math.log(128 / max_exact) * (half - max_exact)).astype(jnp.int32)
        large = jnp.minimum(large, half - 1)
        return np.asarray(ret + jnp.where(n < max_exact, n, large))


def host_tables(NC, rel_bias_table, na_rpb_l=None):
    S = NC * TOK
    rows = S // 64
    NBLK = S // 128
    tab = np.asarray(rel_bias_table, np.float32)
    k = np.arange(128)[:, None]
    out = {}
    q512 = np.arange(512)[None, :]
    relD = np.stack([o * 128 + k - q512 for o in range(-1, 5)])
    bD = _t5_bucket(relD)
    biasD, fb = [], []
    for c in range(NC):
        pc, nx = (c - 1) % NC, (c + 1) % NC
        rel_prev = (pc - c) * TOK + 15 * 128 + k - q512
        rel_next = (nx - c) * TOK + k - (1536 + q512)
        be = _t5_bucket(np.stack([rel_prev, rel_next]))
        bidx = np.concatenate([bD, be], 0)
        biasD.append(np.stack([tab[bidx, h] for h in range(4)]))
        f = np.zeros((4, 2 + NC), np.float32)
        for h in range(4):
            f[h, 0] = tab[15, h]
            f[h, 1] = tab[31, h]
            for i in range(NC):
                kc_ = (c + i) % NC
                f[h, 2 + i] = tab[31, h] if kc_ > c else tab[15, h]
        fb.append(np.broadcast_to(f[None], (128, 4, 2 + NC)).copy())
    out["biasD"], out["fb"] = biasD, fb
    q128 = np.arange(128)[None, :]
    relC = np.stack([o * 128 + k - q128 for o in (-1, 0, 1)])
    bC = _t5_bucket(relC)
    biasC = np.zeros((6, 128, 4, 128), np.float32)
    for g in range(2):
        for oi in range(3):
            for hq in range(4):
                biasC[g * 3 + oi, :, hq, :] = tab[bC[oi], 4 + g * 4 + hq]
    out["biasC"] = biasC.reshape(6, 128, 512)
    okC = (np.abs(relC) <= 128)
    mi = np.where(okC, 0.0, NEG).astype(np.float32)
    maskC = []
    for c in range(NC):
        sets = np.stack([mi, mi, mi])
        if c == 0:
            sets[1, 0] = NEG
        if c == NC - 1:
            sets[2, 2] = NEG
        mc = np.broadcast_to(sets.reshape(9, 128, 1, 128), (9, 128, 4, 128)).reshape(9, 128, 512)
        maskC.append(np.ascontiguousarray(mc))
    out["maskC"] = maskC
    kr2, kcol = (np.arange(128) // 64)[:, None, None], (np.arange(128) % 64)[:, None, None]
    qr2, qcol = (np.arange(128) // 64)[None, None, :], (np.arange(128) % 64)[None, None, :]
    o7 = np.arange(-3, 4)[None, :, None]
    dr_idx = (2 * o7 + kr2 - qr2 + 7) + 0 * kcol + 0 * qcol
    dc_idx = np.clip(kcol - qcol + 15, 0, 30) + 0 * o7
    out["A_idx"] = (np.broadcast_to(dr_idx, (128, 7, 128)), np.broadcast_to(dc_idx, (128, 7, 128)))
    cs = np.clip(qcol - 8, 0, 64 - 16)
    col_ok = (kcol >= cs) & (kcol < cs + 16)

    def maskA_for(j):
        r = 2 * j + qr2
        kr = 2 * (j + o7) + kr2
        rs = np.clip(r - 4, 0, rows - 8)
        ok = (kr >= rs) & (kr < rs + 8) & col_ok
        return np.where(ok, 0.0, NEG).astype(np.float32).reshape(128, 896)

    jmid = NBLK // 2
    maskA = []
    for c in range(NC):
        sets = [maskA_for(jmid)]
        for jl in (0, 1, NT - 2, NT - 1):
            j = c * NT + jl
            sets.append(maskA_for(j) if (j < 2 or j >= NBLK - 2) else maskA_for(jmid))
        maskA.append(np.stack(sets))
    out["maskA"] = maskA
    return out


def gather_biasA(tables, rpb_l):
    dr_idx, dc_idx = tables["A_idx"]
    rpb_l = np.asarray(rpb_l, np.float32)
    return np.ascontiguousarray(rpb_l[:, dr_idx, dc_idx].reshape(8, 128, 896))


_PROG_CACHE = {}


def _prog(key, fn):
    if key not in _PROG_CACHE:
        _PROG_CACHE[key] = fn()
    return _PROG_CACHE[key]


def run_layers(x, w_in, w_branch, w_gate, b_gate, w_o, norm_g, final_g, ffn_w_gate, ffn_w_up, ffn_w_down,
               na_rpb, diff_lambda, diff_subln_g, gqa_sink, rel_bias_table, NC, depth):
    f32 = lambda a: np.ascontiguousarray(np.asarray(a, np.float32))
    x = f32(x).reshape(NC * TOK, D)
    ident = np.eye(128, dtype=np.float32)
    tables = host_tables(NC, rel_bias_table)
    cores = list(range(NC))
    xs = [x[c * TOK:(c + 1) * TOK] for c in cores]
    def ffn_w(l, j, suf=""):
        return {"wg" + suf: f32(ffn_w_gate[l, j]), "wu" + suf: f32(ffn_w_up[l, j]), "wd" + suf: f32(ffn_w_down[l, j])}

    KA_ = _prog("A", build_A)
    in_maps = [{"x": xs[c], "ident": ident, "norm_g": f32(norm_g[0]), "w_in": f32(w_in[0]), **ffn_w(0, 0)} for c in cores]
    res = run_bass_kernel_spmd(KA_.nc, in_maps, core_ids=cores).results
    xs = [res[c]["x1"] for c in cores]
    for l in range(depth):
        comp = {n: np.stack([res[c][n] for c in cores]) for n in ("kT", "vA", "vD", "vC")}
        last = (l == depth - 1)
        KB_ = _prog(("B", NC, last), lambda: build_B(NC, last))
        lam_init = 0.8 - 0.6 * math.exp(-0.3 * l)
        biasA = gather_biasA(tables, na_rpb[l])
        shared = {
            "ident": ident, "norm_g": f32(norm_g[l]), **ffn_w(l, 1),
            "w_in": f32(w_in[l]), "w_gate": f32(w_gate[l]), "w_branch": f32(w_branch[l]), "w_o": f32(w_o[l]),
            "bgate": np.ascontiguousarray(f32(b_gate[l]).reshape(24, 128).T),
            "lam": f32(diff_lambda[l]).reshape(1, 256),
            "consts": np.array([[lam_init, 1.0 - lam_init]], np.float32),
            "subln": f32(diff_subln_g[l]).reshape(128, 1), "sink": f32(gqa_sink[l]).reshape(1, 8),
            "biasA": biasA, "biasC": tables["biasC"],
        }
        if last:
            shared["final_g"] = f32(final_g).reshape(1, D)
        else:
            shared.update({"norm_g_n": f32(norm_g[l + 1]), "w_in_n": f32(w_in[l + 1]), **ffn_w(l + 1, 0, "_n")})
        in_maps = []
        for c in cores:
            rot = lambda a: np.ascontiguousarray(np.roll(a, -c, axis=0))
            in_maps.append({
                "x": xs[c], **shared,
                "fb": tables["fb"][c], "maskA": tables["maskA"][c], "biasD": tables["biasD"][c], "maskC": tables["maskC"][c],
                "kT_all": rot(comp["kT"]), "vA_all": rot(comp["vA"]), "vD_all": rot(comp["vD"]), "vC_all": rot(comp["vC"]),
            })
        res = run_bass_kernel_spmd(KB_.nc, in_maps, core_ids=cores).results
        xs = [res[c]["xo"] for c in cores]
    return np.concatenate(xs, 0).reshape(1, NC * TOK, D).astype(np.float32)


def kernel(x, w_in, w_branch, w_gate, b_gate, w_o, norm_g, final_g, ffn_w_gate, ffn_w_up, ffn_w_down,
           na_rpb, diff_lambda, diff_subln_g, gqa_sink, rel_bias_table):
    return run_layers(x, w_in, w_branch, w_gate, b_gate, w_o, norm_g, final_g, ffn_w_gate, ffn_w_up, ffn_w_down,
                      na_rpb, diff_lambda, diff_subln_g, gqa_sink, rel_bias_table, NCORES, DEPTH)
```
